# Optimizing a Trainium2 kernel written in Bass

```python
import jax, jax.numpy as jnp
from jax import lax
import numpy as np

D_MODEL = 1024
BATCH = 2
SEQ = 8192
DEPTH = 4

HEAD_DIM = 64
D_MIX = D_MODEL
N_Q_HEADS = 8
N_KV_HEADS = 2
GQA_GROUP = N_Q_HEADS // N_KV_HEADS
ATTN_W = N_Q_HEADS * HEAD_DIM
KV_W = N_KV_HEADS * HEAD_DIM
N_SG_HEADS = 4
SG_W = N_SG_HEADS * HEAD_DIM
SG_CHUNK = 128
N_CONV_GROUPS = 4
CONV_W = N_CONV_GROUPS * HEAD_DIM
CONV_K = 3
D_FF = 2816
Q_BLOCK = 128
GRID_W = 64
ROPE_THETA = 10000.0
AXIS_DIM = HEAD_DIM // 2
EPS = 1e-6
PROJ_SPLITS = (ATTN_W, KV_W, KV_W, SG_W, SG_W, CONV_W, CONV_W, CONV_W)
PROJ_W = sum(PROJ_SPLITS)

kernel_name = "hybrid_parallel_heads_encoder"


def _rmsnorm(x, g):
    x32 = x.astype(jnp.float32)
    y = x32 * lax.rsqrt(jnp.mean(x32 * x32, axis=-1, keepdims=True) + EPS)
    return y.astype(x.dtype) * g


def _dwconv3(x, w):
    xp = jnp.pad(x, ((0, 0), (1, 1), (0, 0)))
    return xp[:, :-2] * w[0] + xp[:, 1:-1] * w[1] + xp[:, 2:] * w[2]


def _axial_angles(seq):
    rows = seq // GRID_W
    row = jnp.broadcast_to(jnp.arange(rows)[:, None], (rows, GRID_W)).reshape(-1)
    col = jnp.broadcast_to(jnp.arange(GRID_W)[None, :], (rows, GRID_W)).reshape(-1)
    inv = 1.0 / (ROPE_THETA ** (jnp.arange(AXIS_DIM // 2, dtype=jnp.float32) * 2.0 / AXIS_DIM))
    ang_r = row.astype(jnp.float32)[:, None] * inv[None, :]
    ang_c = col.astype(jnp.float32)[:, None] * inv[None, :]
    return ang_r, ang_c


def _rotate(x, cos, sin):
    x1, x2 = jnp.split(x, 2, axis=-1)
    return jnp.concatenate([x1 * cos - x2 * sin, x1 * sin + x2 * cos], axis=-1)


def _axial_rope(x, tabs):
    cr, sr, cc, sc = tabs
    return jnp.concatenate([_rotate(x[..., :AXIS_DIM], cr, sr),
                            _rotate(x[..., AXIS_DIM:], cc, sc)], axis=-1)


def _block_attention(q, k, v):
    b, s = q.shape[0], q.shape[1]
    nblk = s // Q_BLOCK
    scale = HEAD_DIM ** -0.5
    qb = q.reshape(b, nblk, Q_BLOCK, N_KV_HEADS, GQA_GROUP, HEAD_DIM).transpose(1, 0, 2, 3, 4, 5)

    def one_block(qblk):
        sc = jnp.einsum('bqgrd,bkgd->bgrqk', qblk, k).astype(jnp.float32) * scale
        p = jax.nn.softmax(sc, axis=-1).astype(v.dtype)
        return jnp.einsum('bgrqk,bkgd->bqgrd', p, v)

    out = lax.map(one_block, qb)
    return out.transpose(1, 0, 2, 3, 4, 5).reshape(b, s, ATTN_W)


def _spatial_gating(u, v, g_v, w_s, b_s):
    b, s = u.shape[0], u.shape[1]
    nchunk = s // SG_CHUNK
    vn = _rmsnorm(v, g_v).reshape(b, nchunk, SG_CHUNK, N_SG_HEADS, HEAD_DIM)
    mixed = jnp.einsum('hpq,bnqhc->bnphc', w_s, vn) + b_s.T[:, :, None]
    return u * mixed.reshape(b, s, SG_W)


def setup_inputs(seed: int = 0) -> dict:
    key = jax.random.key(seed)
    ks = jax.random.split(key, 16)
    f32 = jnp.float32
    nrm = lambda k, shape, s: jax.random.normal(k, shape, f32) * s
    res_scale = (2.0 * DEPTH) ** -0.5
    return {
        "x": jax.random.normal(ks[0], (BATCH, SEQ, D_MODEL), f32),
        "norm1_g": 1.0 + nrm(ks[1], (DEPTH, D_MODEL), 0.02),
        "w_in": nrm(ks[2], (DEPTH, D_MODEL, PROJ_W), D_MODEL ** -0.5),
        "q_norm_g": 1.0 + nrm(ks[3], (DEPTH, HEAD_DIM), 0.02),
        "k_norm_g": 1.0 + nrm(ks[4], (DEPTH, HEAD_DIM), 0.02),
        "sg_norm_g": 1.0 + nrm(ks[5], (DEPTH, SG_W), 0.02),
        "sg_w": nrm(ks[6], (DEPTH, N_SG_HEADS, SG_CHUNK, SG_CHUNK), 0.5 * SG_CHUNK ** -0.5),
        "sg_b": 1.0 + nrm(ks[7], (DEPTH, N_SG_HEADS, SG_CHUNK), 0.1),
        "conv_w": nrm(ks[8], (DEPTH, CONV_K, CONV_W), CONV_K ** -0.5),
        "w_out": nrm(ks[9], (DEPTH, D_MIX, D_MODEL), D_MIX ** -0.5 * res_scale),
        "norm2_g": 1.0 + nrm(ks[10], (DEPTH, D_MODEL), 0.02),
        "ffn_w_up": nrm(ks[11], (DEPTH, D_MODEL, 2 * D_FF), D_MODEL ** -0.5),
        "ffn_conv_w": nrm(ks[12], (DEPTH, CONV_K, 2 * D_FF), CONV_K ** -0.5),
        "ffn_w_down": nrm(ks[13], (DEPTH, D_FF, D_MODEL), D_FF ** -0.5 * res_scale),
    }


def reference(x, norm1_g, w_in, q_norm_g, k_norm_g, sg_norm_g, sg_w, sg_b, conv_w,
              w_out, norm2_g, ffn_w_up, ffn_conv_w, ffn_w_down):
    b, s, _ = x.shape
    ang_r, ang_c = _axial_angles(s)
    dt = x.dtype
    tabs = (jnp.cos(ang_r)[:, None, :].astype(dt), jnp.sin(ang_r)[:, None, :].astype(dt),
            jnp.cos(ang_c)[:, None, :].astype(dt), jnp.sin(ang_c)[:, None, :].astype(dt))
    split_at = np.cumsum(PROJ_SPLITS)[:-1].tolist()

    for l in range(DEPTH):
        h = _rmsnorm(x, norm1_g[l])
        p = h @ w_in[l]
        q, k, v, su, sv, cb, cc, cx = jnp.split(p, split_at, axis=-1)

        q = _rmsnorm(q.reshape(b, s, N_Q_HEADS, HEAD_DIM), q_norm_g[l])
        k = _rmsnorm(k.reshape(b, s, N_KV_HEADS, HEAD_DIM), k_norm_g[l])
        q = _axial_rope(q, tabs)
        k = _axial_rope(k, tabs)
        a_out = _block_attention(q, k, v.reshape(b, s, N_KV_HEADS, HEAD_DIM))

        g_out = _spatial_gating(su, sv, sg_norm_g[l], sg_w[l], sg_b[l])

        c_out = cb * _dwconv3(cc * cx, conv_w[l])

        mix = jnp.concatenate([a_out, g_out, c_out], axis=-1)
        x = x + mix @ w_out[l]

        h2 = _rmsnorm(x, norm2_g[l])
        up = _dwconv3(h2 @ ffn_w_up[l], ffn_conv_w[l])
        gate, val = jnp.split(up, 2, axis=-1)
        x = x + (jax.nn.silu(gate) * val) @ ffn_w_down[l]
    return x
```

```python
import numpy as np
from contextlib import ExitStack
import concourse.bass as bass
import concourse.mybir as mybir
from concourse.bass_utils import run_bass_kernel_spmd

F32 = mybir.dt.float32
BF16 = mybir.dt.bfloat16
AF = mybir.ActivationFunctionType
ALU = mybir.AluOpType
AX = mybir.AxisListType

L_ALL = 4
D = 1024
NT = 2048
TC = 512
NPROJ = 2048
DFF = 2816
NFF = 22
EPS = 1e-6
NCORE = 8
GROUPS = [[0, 1, 2, 3], [4, 5, 6, 7]]

O_G1 = 0
O_G2 = 32
O_GQ = 64
O_GK = 68
O_CW = 72
O_FCW = 96
O_MASK = 624
O_QKG = 632
NPP = O_QKG + L_ALL * 128 + 8

SEND1_W = 4144
V_OFF = 2048
E_OFF = 4128


class Op:
    __slots__ = ("eng", "fn", "deps", "kind", "sig", "sigval", "sem", "idx")

    def __init__(self, eng, fn, kind):
        self.eng = eng
        self.fn = fn
        self.kind = kind
        self.deps = []
        self.sig = kind != "c"
        self.sigval = None
        self.sem = None


class Prog:
    ENGS = ("sync", "scalar", "vector", "gpsimd", "tensor")

    def __init__(self):
        self.ops = {e: [] for e in self.ENGS}
        self.lw = {}
        self.lr = {}
        self.consts = set()

    def add(self, eng, fn, r=(), w=(), kind="c"):
        op = Op(eng, fn, kind)
        deps = set()
        for k in r:
            d = self.lw.get(k)
            if d is not None:
                deps.add(d)
        for k in w:
            d = self.lw.get(k)
            if d is not None:
                deps.add(d)
            deps.update(self.lr.get(k, ()))
        for k in w:
            self.lw[k] = op
            self.lr[k] = []
        for k in r:
            if k in self.consts:
                continue
            lst = self.lr.setdefault(k, [])
            if kind == "c":
                lst[:] = [o for o in lst if not (o.kind == "c" and o.eng == eng)]
            lst.append(op)
        for d in deps:
            if d is op:
                continue
            if d.kind == "c" and d.eng == "tensor" and eng == "tensor" and kind == "c":
                continue
            op.deps.append(d)
            d.sig = True
        self.ops[eng].append(op)
        return op

    def alias(self, new_keys, old_keys):
        acc = []
        for k in old_keys:
            d = self.lw.pop(k, None)
            if d is not None:
                acc.append(d)
            acc.extend(self.lr.pop(k, ()))
        for k in new_keys:
            self.lw.pop(k, None)
            self.lr[k] = list(acc)

    def emit(self, nc, es):
        R = 8
        sems = {}
        for e in ("scalar", "vector", "gpsimd", "tensor"):
            sems[e] = es.enter_context(nc.semaphore("c_" + e))
        pools = {e: [es.enter_context(nc.semaphore("d_%s%d" % (e, i))) for i in range(R)]
                 for e in ("sync", "gpsimd", "scalar")}
        CCR = 4
        ccsems = [es.enter_context(nc.semaphore("ccsem%d" % i)) for i in range(CCR)]
        cccount = 0
        for e in self.ENGS:
            cnt = 0
            dcnt = 0
            for op in self.ops[e]:
                if op.kind == "c":
                    if op.sig:
                        cnt += 1
                        op.sem = sems[e]
                        op.sigval = cnt
                elif op.kind == "d":
                    op.sem = pools[e][dcnt % R]
                    op.sigval = 16 * (dcnt // R + 1)
                    dcnt += 1
                else:
                    op.sem = ccsems[cccount % CCR]
                    op.sigval = cccount // CCR + 1
                    cccount += 1
        block = es.enter_context(nc.Block())
        prog = self

        def run(eng_name):
            def body(e):
                waited = {}
                for op in prog.ops[eng_name]:
                    need = {}
                    for d in op.deps:
                        key = id(d.sem)
                        if need.get(key, (None, 0))[1] < d.sigval:
                            need[key] = (d.sem, d.sigval)
                    if op.kind == "d" and op.sigval > 16:
                        key = id(op.sem)
                        if need.get(key, (None, 0))[1] < op.sigval - 16:
                            need[key] = (op.sem, op.sigval - 16)
                    if op.kind == "cc" and op.sigval > 1:
                        key = id(op.sem)
                        if need.get(key, (None, 0))[1] < op.sigval - 1:
                            need[key] = (op.sem, op.sigval - 1)
                    for key, (s, v) in need.items():
                        if waited.get(key, 0) >= v:
                            continue
                        e.wait_ge(s, v)
                        waited[key] = v
                    ins = op.fn(e)
                    if op.sig:
                        ins.then_inc(op.sem, 16 if op.kind == "d" else 1)
            return body

        block.sync(run("sync"))
        block.scalar(run("scalar"))
        block.vector(run("vector"))
        block.gpsimd(run("gpsimd"))
        block.tensor(run("tensor"))


def build_program(n_layers=L_ALL, stop=None):
    nc = bass.Bass("TRN2", target_bir_lowering=False)
    P = Prog()
    es = ExitStack()

    def din(name, shape, dt=F32):
        return nc.dram_tensor(name, list(shape), dt, kind="ExternalInput").ap()

    xT = din("xT", [D, NT])
    w_in = din("w_in", [L_ALL, D, NPROJ])
    w_out = din("w_out", [L_ALL, D, D])
    w_up = din("w_up", [L_ALL, D, 2 * DFF])
    w_down = din("w_down", [L_ALL, DFF, D])
    pp = din("pp", [128, NPP])
    ppl = din("ppl", [L_ALL, 128, 512])
    sgwT = din("sgwT", [L_ALL, 128, 512])
    rope = din("rope", [128, 2 * NT])
    rotm = din("rotm", [128, 128])
    yT = nc.dram_tensor("yT", [D, NT], F32, kind="ExternalOutput").ap()
    send1 = [nc.dram_tensor("send1_%d" % l, [128, NT], BF16).ap() for l in range(n_layers)]
    recv1 = [nc.dram_tensor("recv1_%d" % l, [4 * 128, NT], BF16).ap() for l in range(n_layers)]
    send1b = [nc.dram_tensor("send1b_%d" % l, [128, 2096], BF16).ap() for l in range(n_layers)]
    recv1b = [nc.dram_tensor("recv1b_%d" % l, [4 * 128, 2096], BF16).ap() for l in range(n_layers)]
    send2 = [nc.dram_tensor("send2_%d" % l, [128, 32], BF16).ap() for l in range(n_layers)]
    recv2 = [nc.dram_tensor("recv2_%d" % l, [4 * 128, 32], BF16).ap() for l in range(n_layers)]

    lscr = [nc.dram_tensor("lscr%d" % g, [1, 512], F32).ap() for g in range(2)]

    def sb(name, n, dt):
        return es.enter_context(nc.sbuf_tensor(name, [128, n], dt))

    X_t = sb("X", 8 * NT, F32)
    HB_t = sb("HB", 8192, BF16)
    AR_t = sb("AR", 22528, BF16)
    Q_t = sb("Q", 4 * NT, BF16)
    CB_t = sb("CB", 2 * NT, BF16)
    GO_t = sb("GO", 2 * NT, BF16)
    RC_t = sb("RC", 8192 + 32, BF16)
    PP_t = sb("PP", NPP, F32)
    PL_t = sb("PL", 512, F32)
    SGW_t = sb("SGW", 512, BF16)
    CST_t = sb("CST", 384, BF16)
    ONEF_t = sb("ONEF", 64, F32)
    BIAS_t = sb("BIAS", 4, F32)
    EPSB_t = sb("EPSB", 1, F32)
    TB_t = sb("TB", 4096, BF16)
    TF_t = sb("TF", 4608, F32)

    def V(t, off, *shape):
        n = int(np.prod(shape))
        a = t[:, off:off + n]
        if len(shape) == 1:
            return a
        if len(shape) == 2:
            return a.rearrange("p (a b) -> p a b", a=shape[0])
        if len(shape) == 3:
            return a.rearrange("p (a b c) -> p a b c", a=shape[0], b=shape[1])
        return a.rearrange("p (a b c d) -> p a b c d", a=shape[0], b=shape[1], c=shape[2])

    X = V(X_t, 0, 8, NT)
    Q = V(Q_t, 0, 4, NT)
    Qn = V(Q_t, 0, 16, 4, 128)
    CB = V(CB_t, 0, 2, NT)
    GO = V(GO_t, 0, 2, NT)
    ROPE = V(RC_t, 0, 2, NT)
    OT = V(RC_t, 0, 8, TC)
    CCX = V(RC_t, 4096, 2, NT + 2)
    EDG = V(RC_t, 8208, 4)
    EDG16 = V(RC_t, 8208, 16)
    ONES = CST_t[:, 0:128]
    BO = CST_t[:, 128:256]
    ROT = CST_t[:, 256:384]
    PPv = PP_t
    WI = V(AR_t, 0, 8, NPROJ)
    KO = V(AR_t, 16384, NT)
    VO = V(AR_t, 18432, 16, 2, 65)
    KA = V(AR_t, 0, 4, NT)
    VA = V(AR_t, 8192, 64, 2, 65)
    WOG = V(AR_t, 16512, 4, D)
    AT = V(AR_t, 0, NFF, 1024)
    WOA = V(HB_t, 0, 8, D)
    H = V(HB_t, 0, 2, 8, TC)
    H2 = V(HB_t, 0, 8, 1024)
    WU = V(Q_t, 0, 2, 2, 8, 256)
    WD = [V(CB_t, 0, NFF, 128), V(GO_t, 0, NFF, 128)]
    SU = V(TF_t, 3072, 2, 512)
    TSM = TF_t[:, 4104:4608]
    U = [V(TF_t, 0, 1026), V(TF_t, 1026, 1026)]
    CV = [V(TF_t, 2052, 1024), V(TF_t, 3076, 1024)]
    RCF = RC_t[:, 0:8208].bitcast(F32)
    U2 = [RCF[:, 0:1026], RCF[:, 1026:2052]]
    CV2 = [RCF[:, 2052:3076], RCF[:, 3076:4100]]
    RCK_F = [("U2", 0), ("U2", 1), ("CV2", 0), ("CV2", 1)]
    TFK = [("TF", i) for i in range(8)]
    UCK = [("U", 0), ("U", 1), ("CV", 0), ("CV", 1)]
    QCG = ["Q%d" % n for n in range(4)] + ["CBn%d" % n for n in range(4)] + ["GOn%d" % n for n in range(4)]
    WUD = [("WU", s_, g_) for s_ in range(2) for g_ in range(2)] + ["WD0", "WD1"]
    ARA = ["WI%d" % k for k in range(8)] + ["KO", "VO"]
    ARC = [("KA", r) for r in range(4)] + [("VA", r) for r in range(4)] + ["WOG"]

    def dma(eng, out, in_, r, w):
        return P.add(eng, lambda e: e.dma_start(out=out, in_=in_), r, w, kind="d")

    def mm(out, lhsT, rhs, start, stop, r, w):
        return P.add("tensor", lambda e: e.matmul(out, lhsT, rhs, start=start, stop=stop), r, w)

    def act(out, in_, func, r, w, bias=0.0, scale=1.0):
        return P.add("scalar", lambda e: e.activation(out=out, in_=in_, func=func, bias=bias, scale=scale), r, w)

    def tt(eng, out, a, b, op, r, w):
        return P.add(eng, lambda e: e.tensor_tensor(out=out, in0=a, in1=b, op=op), r, w)

    def ts(eng, out, a, s1, op0, r, w, s2=None, op1=None):
        if op1 is None:
            return P.add(eng, lambda e: e.tensor_scalar(out=out, in0=a, scalar1=s1, scalar2=None, op0=op0), r, w)
        return P.add(eng, lambda e: e.tensor_scalar(out=out, in0=a, scalar1=s1, scalar2=s2, op0=op0, op1=op1), r, w)

    def stt(out, a, s, b, op0, op1, r, w):
        return P.add("vector", lambda e: e.scalar_tensor_tensor(out=out, in0=a, scalar=s, in1=b, op0=op0, op1=op1), r, w)

    def recip(out, in_, r, w):
        return P.add("vector", lambda e: e.reciprocal(out=out, in_=in_), r, w)

    def cp(eng, out, in_, r, w):
        return P.add(eng, lambda e: e.tensor_copy(out=out, in_=in_), r, w)

    def memset(eng, ap, val, w):
        return P.add(eng, lambda e: e.memset(ap, val), (), w)

    def raw(eng, meth, r, w, kind="c", **kw):
        return P.add(eng, lambda e: getattr(e, meth)(**kw), r, w, kind=kind)

    PSALL = es.enter_context(nc.psum_tensor("psall", [128, 4096], F32))
    PS = [PSALL[:, i * 512:(i + 1) * 512] for i in range(8)]

    def pk(i):
        return ("ps", i)

    import os as _os
    for _ in range(int(_os.environ.get("KDUMMY", "0"))):
        P.add("sync", lambda e: e.nop(), (), ())
    dma("sync", PP_t[:, :], pp, (), ["PP"])
    dma("gpsimd", ROT, rotm, (), ["CST"])
    memset("vector", CST_t[:, 0:128], 1.0, ["CST1"])
    memset("vector", CST_t[:, 128:256], 0.0, ["CST2"])
    memset("vector", CST_t[0:64, 128:192], 1.0, ["CST2"])
    memset("vector", CST_t[64:128, 192:256], 1.0, ["CST2"])
    memset("vector", ONEF_t[:, :], 1.0, ["ONEF"])
    memset("vector", EPSB_t[:, :], EPS, ["EPSB"])
    memset("vector", TB_t[:, 2048:3072], 0.0, [("TB", 4), ("TB", 5)])
    memset("vector", TB_t[:, 3072:3584], 0.0, ["H2E"])
    memset("vector", EDG16, 0.0, ["EDG"])
    for c in range(8):
        dma("sync", X[:, c, :], xT[c * 128:(c + 1) * 128, :], (), [("X", c, n) for n in range(4)])
    P.consts.update(["PP", "CST", "CST1", "CST2", "ONEF", "EPSB"])

    tbn = [0]

    def tb():
        i = tbn[0] % 4
        tbn[0] += 1
        return TB_t[:, i * 512:(i + 1) * 512], ("TB", i)

    tfn = [0]

    def tf():
        i = tfn[0] % 6
        tfn[0] += 1
        return TF_t[:, i * 512:(i + 1) * 512], ("TF", i)

    def rms_sumsq(psb, src_fn, nchunk, lhsT, rkeys):
        for c in range(nchunk):
            sq, sqk = tb()
            src, sk = src_fn(c)
            act(sq, src, AF.Square, sk, [sqk])
            mm(PS[psb][:, :], lhsT, sq, c == 0, c == nchunk - 1, [sqk] + rkeys, [pk(psb)])

    def rstd_from(psb, inv_n, buf=None):
        rs, rk = buf if buf is not None else tf()
        act(rs, PS[psb][:, :], AF.Ln, [pk(psb), "EPSB"], [rk], bias=EPSB_t[:, 0:1], scale=inv_n)
        act(rs, rs, AF.Exp, [rk], [rk], scale=-0.5)
        return rs, rk

    def norm_chunk(l, goff, tok0, dst_fn, dstkeys, buf=None):
        n = tok0 // TC
        rms_sumsq(0, lambda c: (X[:, c, tok0:tok0 + TC], [("X", c, n)]), 8, ONES, ["CST1"])
        rs, rk = rstd_from(0, 1.0 / D, buf)
        for c in range(8):
            stt(dst_fn(c), X[:, c, tok0:tok0 + TC], PPv[:, goff + l * 8 + c:goff + l * 8 + c + 1], rs,
                ALU.mult, ALU.mult, [("X", c, n), rk, "PP"], dstkeys(c))

    for l in range(n_layers if stop != 'setup' else 0):
        P.alias(ARA, ["AT"] + ARC)
        P.alias([("H", 0), ("H", 1)], ["H2", "WOA"])
        P.alias(QCG, WUD)
        P.alias(TFK, UCK)
        P.alias([("ROPE", 0), ("ROPE", 1), "CCX"], ["OT", "CCX"] + RCK_F)
        for k in range(8):
            dma("gpsimd", WI[:, k, :], w_in[l, k * 128:(k + 1) * 128, :], (), ["WI%d" % k])
        dma("sync", PL_t[:, :], ppl[l], (), ["PL"])
        dma("gpsimd", SGW_t[:, :], sgwT[l], (), ["SGW"])
        dma("gpsimd", RC_t[:, 0:NT], rope[:, 0:NT], (), [("ROPE", 0)])
        dma("gpsimd", RC_t[:, NT:2 * NT], rope[:, NT:2 * NT], (), [("ROPE", 1)])
        RK = [("ROPE", 0), ("ROPE", 1)]
        VOf = AR_t[:, 18432:18432 + 16 * 130].rearrange("p (a c) -> p a c", c=65)
        memset("vector", VOf[:, :, 64:65], 1.0, ["VO"])
        qkg = PPv[:, O_QKG + l * 128:O_QKG + (l + 1) * 128].rearrange("p (a b) -> p a b", a=2)
        raw("vector", "tensor_reduce", ["PP"], ["TSM"], out=TSM[:, 0:2], in_=qkg, axis=AX.X, op=ALU.max,
            apply_absolute_value=True)
        ts("vector", BIAS_t[:, l:l + 1], TSM[:, 0:1], TSM[:, 1:2], ALU.mult, ["TSM"], [("BIAS", l)], s2=-8.0,
           op1=ALU.mult)

        O_GV = NPP - 8

        def send_block():
            for i in range(2):
                cp("vector", EDG[:, 2 * i:2 * i + 1], CCX[:, i, 1:2], ["CCX"], ["EDG"])
                cp("vector", EDG[:, 2 * i + 1:2 * i + 2], CCX[:, i, NT:NT + 1], ["CCX"], ["EDG"])
            dma("sync", send1[l], KO, ["KO"], [("S1", l, 0)])
            dma("sync", send1b[l][:, 0:2080], AR_t[:, 18432:18432 + 2080], ["VO"], [("S1", l, 1)])
            dma("sync", send1b[l][:, 2080:2096], EDG16, ["EDG"], [("S1", l, 2)])
            P.add("gpsimd", lambda e, l=l: e.collective_compute("AllGather", ALU.bypass, replica_groups=GROUPS,
                                                                ins=[send1[l]], outs=[recv1[l]]),
                  [("S1", l, 0)], [("R1", l)], kind="cc")
            P.add("gpsimd", lambda e, l=l: e.collective_compute("AllGather", ALU.bypass, replica_groups=GROUPS,
                                                                ins=[send1b[l]], outs=[recv1b[l]]),
                  [("S1", l, 1), ("S1", l, 2)], [("R1b", l)], kind="cc")

        for n in range(4):
            tok0 = n * TC
            hb = n % 2
            hk = ("H", hb)
            if n == 0:
                norm_chunk(l, O_G1, tok0, lambda c: H[:, hb, c, :], lambda c: [hk])
            cct = [None, None]
            st = {}
            pbc = [0]

            def stage0(j):
                pbc[0] += 1
                pb = 1 + (pbc[0] % 2)
                for k in range(8):
                    mm(PS[pb][:, :], WI[:, k, j * 128:(j + 1) * 128], H[:, hb, k, :], k == 0, k == 7,
                       [hk, "WI%d" % k], [pk(pb)])
                if j <= 4:
                    qf, qfk = tf()
                    act(qf, PS[pb][:, :], AF.Copy, [pk(pb)], [qfk])
                    sq, sqk = tb()
                    act(sq, qf, AF.Square, [qfk], [sqk])
                    st[j] = dict(qf=qf, qfk=qfk, sq=sq, sqk=sqk)
                elif j <= 6:
                    act(SU[:, j - 5, :], PS[pb][:, :], AF.Copy, [pk(pb)], [("TF", 6 + j - 5)])
                elif j <= 8:
                    act(CB[:, j - 7, tok0:tok0 + TC], PS[pb][:, :], AF.Copy, [pk(pb)], ["CBn%d" % n])
                elif j <= 10:
                    c_, ck_ = tf()
                    act(c_, PS[pb][:, :], AF.Copy, [pk(pb)], [ck_])
                    cct[j - 9] = (c_, ck_)
                else:
                    i = j - 11
                    c_, ck_ = cct[i]
                    tt("vector", CCX[:, i, 1 + tok0:1 + tok0 + TC], PS[pb][:, :], c_, ALU.mult,
                       [pk(pb), ck_], ["CCX"])

            def stage1(j):
                d_ = st[j]
                gcol = (O_GQ if j < 4 else O_GK) + l
                mm(PS[3][:, :], BO, d_["sq"], True, True, [d_["sqk"], "CST2"], [pk(3)])
                rs, rk = rstd_from(3, 1.0 / 64)
                qn, qnk = tb()
                stt(qn, d_["qf"], PPv[:, gcol:gcol + 1], rs, ALU.mult, ALU.mult, [d_["qfk"], rk, "PP"], [qnk])
                d_["qn"], d_["qnk"] = qn, qnk

            def stage2(j):
                d_ = st[j]
                qn, qnk = d_["qn"], d_["qnk"]
                mm(PS[4][:, :], ROT, qn, True, True, [qnk, "CST"], [pk(4)])
                t1, t1k = tf()
                tt("gpsimd", t1, qn, ROPE[:, 0, tok0:tok0 + TC], ALU.mult, [qnk] + RK, [t1k])
                t2, t2k = tf()
                tt("vector", t2, PS[4][:, :], ROPE[:, 1, tok0:tok0 + TC], ALU.mult, [pk(4)] + RK, [t2k])
                if j < 4:
                    tt("vector", Qn[:, n * 4:(n + 1) * 4, j, :], t1.rearrange("p (b q) -> p b q", b=4),
                       t2.rearrange("p (b q) -> p b q", b=4), ALU.add, [t1k, t2k], ["Q%d" % n])
                else:
                    tt("vector", KO[:, tok0:tok0 + TC], t1, t2, ALU.add, [t1k, t2k], ["KO"])

            def run_steps(order):
                L_ = len(order)
                for s_ in range(L_ + 2):
                    if s_ < L_:
                        stage0(order[s_])
                    if 0 <= s_ - 1 < L_ and order[s_ - 1] <= 4:
                        stage1(order[s_ - 1])
                    if 0 <= s_ - 2 < L_ and order[s_ - 2] <= 4:
                        stage2(order[s_ - 2])

            run_steps([4, 9, 10, 11, 12])
            def tm_a(t4):
                tile = n * 4 + t4
                tmb = 5 if tile % 2 == 0 else 0
                for k in range(8):
                    mm(PS[tmb][:, 0:384], H[:, hb, k, t4 * 128:(t4 + 1) * 128], WI[:, k, 1664:2048], k == 0, k == 7,
                       [hk, "WI%d" % k], [pk(tmb)])
                sq, sqk = tf()
                act(sq[:, 0:256], PS[tmb][:, 128:384], AF.Square, [pk(tmb)], [sqk])
                act(VO[:, tile, :, 0:64], PS[tmb][:, 0:128].rearrange("p (g c) -> p g c", g=2), AF.Copy,
                    [pk(tmb)], ["VO"])
                ssk = ("SS", t4 % 2)
                ss = TSM[:, 8 + (t4 % 2) * 4: 9 + (t4 % 2) * 4]
                raw("vector", "tensor_reduce", [sqk], [ssk], out=ss, in_=sq[:, 0:256], axis=AX.X, op=ALU.add)
                act(ss, ss, AF.Ln, [ssk, "EPSB"], [ssk], bias=EPSB_t[:, 0:1], scale=1.0 / 256)
                act(ss, ss, AF.Exp, [ssk], [ssk], scale=-0.5)
                vb = tile % 2
                vnp = V(TB_t, 2048 + vb * 512, 2, 2, 128)
                vnk = ("TB", 4 + vb)
                for a in range(2):
                    ts("vector", vnp[:, :, a, a * 64:(a + 1) * 64],
                       PS[tmb][:, 128:384].rearrange("p (i a c) -> p i a c", i=2, a=2)[:, :, a, :], ss,
                       ALU.mult, [pk(tmb), ssk], [vnk])

            def tm_b(t4):
                tile = n * 4 + t4
                vb = tile % 2
                vnp = V(TB_t, 2048 + vb * 512, 2, 2, 128)
                vnk = ("TB", 4 + vb)
                for i in range(2):
                    for a in range(2):
                        mm(PS[6 + i][:, t4 * 128:(t4 + 1) * 128], vnp[:, i, a, :],
                           SGW_t[:, (2 * i + a) * 128:(2 * i + a + 1) * 128], a == 0, a == 1,
                           [vnk, "SGW"], [pk(6 + i)])

            for t4 in range(5):
                if t4 < 4:
                    tm_a(t4)
                if t4 >= 1:
                    tm_b(t4 - 1)
            if n == 3:
                send_block()
            else:
                nb = (n + 1) % 2
                norm_chunk(l, O_G1, (n + 1) * TC, lambda c, nb=nb: H[:, nb, c, :], lambda c, nb=nb: [("H", nb)])
            run_steps([0, 1, 2, 3, 5, 6, 7, 8])
            for i in range(2):
                t_, tk_ = tf()
                b2 = PL_t[:, i * 128:(i + 1) * 128]
                gvs = PPv[:, O_GV + l * 2 + i:O_GV + l * 2 + i + 1]
                for t4 in range(4):
                    stt(t_[:, t4 * 128:(t4 + 1) * 128], PS[6 + i][:, t4 * 128:(t4 + 1) * 128], gvs, b2,
                        ALU.mult, ALU.add, [pk(6 + i), "PL", "PP"], [tk_])
                tt("vector", GO[:, i, tok0:tok0 + TC], t_, SU[:, i, :], ALU.mult, [tk_, ("TF", 6 + i)],
                   ["GOn%d" % n])

        if stop == 'A':
            break
        P.alias(ARC, ARA)
        r1 = recv1[l].rearrange("(r p) c -> p r c", p=128)
        r1b = recv1b[l].rearrange("(r p) c -> p r c", p=128)
        for r in range(4):
            dma("sync", KA[:, r, :], r1[:, r, :], [("R1", l)], [("KA", r)])
            dma("scalar", AR_t[:, 8192 + r * 2080: 8192 + (r + 1) * 2080], r1b[:, r, 0:2080],
                [("R1b", l)], [("VA", r)])
        E4 = V(TB_t, 3072, 4, 4)
        dma("sync", E4, r1b[:, :, 2080:2084], [("R1b", l)], ["E4"])
        if stop == 'B2':
            break
        P.alias(["WOA"], [("H", 0), ("H", 1)])
        wo = w_out[l]
        dma("gpsimd", WOA[0:64, :, :], wo[0:512, :].rearrange("(h p) d -> p h d", p=64), (), ["WOA"])
        dma("gpsimd", WOG, wo[512:1024, :].rearrange("(c p) d -> p c d", p=128), (), ["WOG"])
        if stop == 'B3':
            break
        hal = TSM[:, 16:20]
        for side in range(2):
            e_idx = 1 if side == 0 else 0

            def src(r, e_idx=e_idx):
                return E4[:, r, :].rearrange("p (i e) -> p i e", i=2)[:, :, e_idx]
            dst = hal[:, side * 2:side * 2 + 2]
            mcol = O_MASK + side * 4
            ts("vector", dst, src(0), PPv[:, mcol:mcol + 1], ALU.mult, ["E4", "PP"], ["HAL"])
            for r in range(1, 4):
                stt(dst, src(r), PPv[:, mcol + r:mcol + r + 1], dst, ALU.mult, ALU.add, ["E4", "PP", "HAL"], ["HAL"])
        for i in range(2):
            cp("vector", CCX[:, i, 0:1], hal[:, i:i + 1], ["HAL"], ["CCX"])
            cp("vector", CCX[:, i, NT + 1:NT + 2], hal[:, 2 + i:3 + i], ["HAL"], ["CCX"])
        for n in range(4):
            tok0 = n * TC
            for i in range(2):
                cw = O_CW + l * 6 + i * 3
                t_, tk_ = tf()
                ts("vector", t_, CCX[:, i, tok0:tok0 + TC], PPv[:, cw:cw + 1], ALU.mult, ["CCX", "PP"], [tk_])
                stt(t_, CCX[:, i, tok0 + 1:tok0 + 1 + TC], PPv[:, cw + 1:cw + 2], t_, ALU.mult, ALU.add,
                    ["CCX", "PP", tk_], [tk_])
                stt(t_, CCX[:, i, tok0 + 2:tok0 + 2 + TC], PPv[:, cw + 2:cw + 3], t_, ALU.mult, ALU.add,
                    ["CCX", "PP", tk_], [tk_])
                tt("vector", CB[:, i, tok0:tok0 + TC], CB[:, i, tok0:tok0 + TC], t_, ALU.mult,
                   ["CBn%d" % n, tk_], ["CBn%d" % n])

        if stop == 'B':
            break
        P.alias(["OT"], RK)

        ecols = [0, 1023, 1024, 2047]

        def edge_norm(es, slot, psb):
            o = 128 + slot * 96
            XE = TSM[:, o:o + 32].rearrange("p (c e) -> p c e", c=8)
            xg = TSM[:, o + 32:o + 64].rearrange("p (c e) -> p c e", c=8)
            rse = TSM[:, o + 64:o + 68]
            sqe = TB_t[:, 3312 + slot * 32:3344 + slot * 32]
            H2E_ = TB_t[:, 3120:3152].rearrange("p (c e) -> p c e", c=8)
            kx, ks, kr, kg = ("XE", slot), ("SQE", slot), ("RSE", slot), ("XG", slot)
            memset("vector", TSM[:, o:o + 32], 1.0, [kx])
            for e_i in es:
                ec = ecols[e_i]
                cp("vector", XE[:, :, e_i], X[:, :, ec], [("X", c, ec // TC) for c in range(8)], [kx])
            act(sqe, TSM[:, o:o + 32], AF.Square, [kx], [ks])
            sqe3 = sqe.rearrange("p (c e) -> p c e", c=8)
            for c in range(8):
                mm(PS[psb][:, 0:4], ONES, sqe3[:, c, :], c == 0, c == 7, [ks, "CST1"], [pk(psb)])
            act(rse, PS[psb][:, 0:4], AF.Ln, [pk(psb), "EPSB"], [kr], bias=EPSB_t[:, 0:1], scale=1.0 / D)
            act(rse, rse, AF.Exp, [kr], [kr], scale=-0.5)
            g2 = PPv[:, O_G2 + l * 8:O_G2 + l * 8 + 8]
            for e_i in es:
                tt("vector", xg[:, :, e_i], XE[:, :, e_i], g2, ALU.mult, [kx, "PP"], [kg])
                ts("vector", H2E_[:, :, e_i], xg[:, :, e_i], rse[:, e_i:e_i + 1], ALU.mult, [kg, kr], ["H2E"])

        def edge_exchange():
            edge_norm([0, 3], 0, 6)
            dma("sync", send2[l], TB_t[:, 3120:3152], ["H2E"], [("S2", l)])
            P.add("gpsimd", lambda e, l=l: e.collective_compute("AllGather", ALU.bypass, replica_groups=GROUPS,
                                                                ins=[send2[l]], outs=[recv2[l]]),
                  [("S2", l)], [("R2", l)], kind="cc")
            dma("sync", TB_t[:, 3152:3280].rearrange("p (r c) -> p r c", r=4),
                recv2[l].rearrange("(r p) c -> p r c", p=128), [("R2", l)], ["E2"])

        pend = []

        def wout_closures(n, d):
            tok0 = n * TC
            pb = 7
            cl = []
            for h in range(8):
                cl.append(lambda h=h: mm(PS[pb][:, :], WOA[0:64, h, d * 128:(d + 1) * 128], OT[0:64, h, :],
                                         h == 0, False, ["WOA", "OT"], [pk(pb)]))
            for i in range(2):
                cl.append(lambda i=i: mm(PS[pb][:, :], WOG[:, i, d * 128:(d + 1) * 128], GO[:, i, tok0:tok0 + TC],
                                         False, False, ["WOG", "GOn%d" % n], [pk(pb)]))

            def last(i):
                mm(PS[pb][:, :], WOG[:, 2 + i, d * 128:(d + 1) * 128], CB[:, i, tok0:tok0 + TC], False, i == 1,
                   ["WOG", "CBn%d" % n], [pk(pb)])
                if i == 1:
                    tt("vector", X[:, d, tok0:tok0 + TC], X[:, d, tok0:tok0 + TC], PS[pb][:, :], ALU.add,
                       [("X", d, n), pk(pb)], [("X", d, n)])
            for i in range(2):
                cl.append(lambda i=i: last(i))
            return cl

        LB = V(TF_t, 1024, 2, 512)

        def evac(n, qb):
            for g in range(2):
                os_ = TF_t[:, g * 512:(g + 1) * 512]
                cp("vector", os_[0:65, :], PS[4 + g][0:65, :], [pk(4 + g)], [("TF", g)])
            for g in range(2):
                os_ = TF_t[:, g * 512:(g + 1) * 512]
                osk = ("TF", g)
                lbk = ("TF", 2 + g)
                dma("sync", lscr[g], os_[64:65, :], [osk], [("LD", g)])
                dma("sync", LB[0:64, g, :], lscr[g].partition_broadcast(64), [("LD", g)], [lbk])
                recip(LB[0:64, g, :], LB[0:64, g, :], [lbk], [lbk])
                tt("vector", OT[0:64, 4 * g:4 * g + 4, qb * 128:(qb + 1) * 128],
                   os_[0:64, :].rearrange("p (h q) -> p h q", h=4),
                   LB[0:64, g, :].rearrange("p (h q) -> p h q", h=4), ALU.mult, [osk, lbk], ["OT"])

        items = [(n, qb, kt) for n in (0, 3, 1, 2) for qb in range(4) for kt in range(64)]

        def emit_qk(it):
            n, qb, kt = items[it]
            q0 = n * TC + qb * 128
            sp = it % 2
            for g in range(2):
                mm(PS[sp * 2 + g][:, :].rearrange("p (h q) -> p h q", h=4),
                   KA[64 * g:64 * g + 64, kt // 16, (kt % 16) * 128:(kt % 16 + 1) * 128],
                   Qn[64 * g:64 * g + 64, n * 4 + qb, :, :], True, True, [("KA", kt // 16), "Q%d" % n],
                   [("sp", sp)])

        PTB = [TB_t[:, 0:1024], TB_t[:, 1024:2048], TF_t[:, 2048:2560].bitcast(BF16)]
        PTK = [[("TB", 0), ("TB", 1)], [("TB", 2), ("TB", 3)], [("TF", 4)]]

        def emit_exp(it):
            sp = it % 2
            pb_ = it % 3
            act(PTB[pb_], PSALL[:, sp * 1024:(sp + 1) * 1024], AF.Exp, [("sp", sp)], PTK[pb_], scale=0.125)

        def emit_pv(it):
            n, qb, kt = items[it]
            pb_ = it % 3
            for g in range(2):
                mm(PS[4 + g][0:65, :], VA[:, kt, g, :], PTB[pb_][:, g * 512:(g + 1) * 512],
                   kt == 0, kt == 63, [("VA", kt // 16)] + PTK[pb_], [pk(4 + g)])
            if kt == 63:
                while pend:
                    pend.pop(0)()
                evac(n, qb)
                if qb == 3:
                    for d in range(8):
                        pend.extend(wout_closures(n, d))
                    if n == 3:
                        pend.append(edge_exchange)

        P.alias([("sp", 0), ("sp", 1)], [pk(0), pk(1), pk(2), pk(3)])
        nit = len(items)
        emit_qk(0)
        emit_qk(1)
        emit_exp(0)
        emit_exp(1)
        for it in range(nit):
            if it + 2 < nit:
                emit_qk(it + 2)
            emit_pv(it)
            if it + 2 < nit:
                emit_exp(it + 2)
            kt = items[it][2]
            if kt >= 14:
                for _ in range(2):
                    if pend:
                        pend.pop(0)()
        while pend:
            pend.pop(0)()
        P.alias([pk(0), pk(1), pk(2), pk(3)], [("sp", 0), ("sp", 1)])

        if stop == 'C':
            break
        P.alias(["H2"], ["WOA"])
        P.alias(["AT"], ARC)
        P.alias(WUD, QCG)
        P.alias(UCK, TFK)
        P.alias(RCK_F, ["OT", "CCX"])
        wu_list = [(hf, b) for hf in range(2) for b in range(11)]
        wd_list = [(hf, d) for hf in range(2) for d in range(8)]
        wu_next = [0]
        wd_next = [0]

        def issue_wu(upto):
            while wu_next[0] <= upto and wu_next[0] < len(wu_list):
                q_ = wu_next[0]
                hf_, b_ = wu_list[q_]
                st_ = q_ % 2
                for gv in range(2):
                    col0 = gv * DFF + b_ * 256
                    dma("gpsimd", WU[:, st_, gv, :, :],
                        w_up[l, :, col0:col0 + 256].rearrange("(k p) n -> p k n", p=128), (), [("WU", st_, gv)])
                wu_next[0] += 1

        def issue_wd(upto):
            while wd_next[0] <= upto and wd_next[0] < len(wd_list):
                q_ = wd_next[0]
                hf_, d_ = wd_list[q_]
                st_ = q_ % 2
                dma("gpsimd", WD[st_], w_down[l, :, d_ * 128:(d_ + 1) * 128].rearrange("(j p) n -> p j n", p=128),
                    (), ["WD%d" % st_])
                wd_next[0] += 1

        issue_wu(1)
        issue_wd(1)
        edge_norm([1, 2], 1, 0)
        E2 = TB_t[:, 3152:3280].rearrange("p (r c e) -> p r c e", r=4, c=8)
        H2E = TB_t[:, 3120:3152].rearrange("p (c e) -> p c e", c=8)
        halh = TSM[:, 96:112].rearrange("p (s c) -> p s c", s=2)
        for side in range(2):
            e_idx = 3 if side == 0 else 0
            mcol = O_MASK + side * 4
            ts("vector", halh[:, side, :], E2[:, 0, :, e_idx], PPv[:, mcol:mcol + 1], ALU.mult, ["E2", "PP"], ["HALH"])
            for r in range(1, 4):
                stt(halh[:, side, :], E2[:, r, :, e_idx], PPv[:, mcol + r:mcol + r + 1], halh[:, side, :],
                    ALU.mult, ALU.add, ["E2", "PP", "HALH"], ["HALH"])
        HH = TB_t[:, 3280:3312].rearrange("p (f k e) -> p f k e", f=2, k=8)
        cp("vector", HH[:, 0, :, 0], halh[:, 0, :], ["HALH"], ["HH"])
        cp("vector", HH[:, 0, :, 1], H2E[:, :, 2], ["H2E"], ["HH"])
        cp("vector", HH[:, 1, :, 0], H2E[:, :, 1], ["H2E"], ["HH"])
        cp("vector", HH[:, 1, :, 1], halh[:, 1, :], ["HALH"], ["HH"])

        epi_pend = []
        for hf in range(2):
            h0 = hf * 1024
            for t in range(2):
                norm_chunk(l, O_G2, h0 + t * TC, lambda c, t=t: H2[:, c, t * TC:(t + 1) * TC], lambda c: ["H2"],
                           buf=(CV[1][:, 0:512], ("CV", 1)))
            for b in range(11):
                q_ = hf * 11 + b
                st = q_ % 2
                issue_wu(q_ + 1)
                for jj in range(2):
                    j = 2 * b + jj
                    for gv in range(2):
                        wuk = ("WU", st, gv)
                        for t in range(2):
                            pb = 1 + gv * 2 + t
                            for k in range(8):
                                mm(PS[pb][:, :], WU[:, st, gv, k, jj * 128:(jj + 1) * 128],
                                   H2[:, k, t * TC:(t + 1) * TC], k == 0, k == 7, [wuk, "H2"], [pk(pb)])
                        hbk = 5 if j % 2 == 0 else 0
                        for k in range(8):
                            mm(PS[hbk][:, gv * 2:gv * 2 + 2], WU[:, st, gv, k, jj * 128:(jj + 1) * 128],
                               HH[:, hf, k, :], k == 0, k == 7, [wuk, "HH"], [pk(hbk)])
                    bs = j % 2
                    Ub, CVb = (U, CV) if bs == 0 else (U2, CV2)
                    for gv in range(2):
                        uk = ("U", gv) if bs == 0 else ("U2", gv)
                        act(Ub[gv][:, 1:513], PS[1 + gv * 2][:, :], AF.Copy, [pk(1 + gv * 2)], [uk])
                        act(Ub[gv][:, 513:1025], PS[2 + gv * 2][:, :], AF.Copy, [pk(2 + gv * 2)], [uk])
                        hbk = 5 if j % 2 == 0 else 0
                        cp("vector", Ub[gv][:, 0:1], PS[hbk][:, gv * 2:gv * 2 + 1], [pk(hbk)], [uk])
                        cp("vector", Ub[gv][:, 1025:1026], PS[hbk][:, gv * 2 + 1:gv * 2 + 2], [pk(hbk)], [uk])
                        fw = O_FCW + (l * 44 + gv * 22 + j) * 3
                        ck = ("CV", gv) if bs == 0 else ("CV2", gv)
                        act(CVb[gv], Ub[gv][:, 0:1024], AF.Copy, [uk, "PP"], [ck], scale=PPv[:, fw:fw + 1])
                        stt(CVb[gv], Ub[gv][:, 1:1025], PPv[:, fw + 1:fw + 2], CVb[gv], ALU.mult, ALU.add,
                            [uk, "PP", ck], [ck])
                        stt(CVb[gv], Ub[gv][:, 2:1026], PPv[:, fw + 2:fw + 3], CVb[gv], ALU.mult, ALU.add,
                            [uk, "PP", ck], [ck])
                    ck0 = ("CV", 0) if bs == 0 else ("CV2", 0)
                    ck1 = ("CV", 1) if bs == 0 else ("CV2", 1)

                    def epi(CVb=CVb, ck0=ck0, ck1=ck1, j=j):
                        act(CVb[0], CVb[0], AF.Silu, [ck0], [ck0])
                        tt("vector", AT[:, j, :], CVb[0], CVb[1], ALU.mult, [ck0, ck1], ["AT"])
                    if epi_pend:
                        epi_pend.pop(0)()
                    epi_pend.append(epi)
            while epi_pend:
                epi_pend.pop(0)()
            for d in range(8):
                q_ = hf * 8 + d
                st = q_ % 2
                wdk = "WD%d" % st
                issue_wd(q_ + 1)
                for t in range(2):
                    pb = 6 + t
                    for j in range(NFF):
                        mm(PS[pb][:, :], WD[st][:, j, :], AT[:, j, t * TC:(t + 1) * TC], j == 0, j == NFF - 1,
                           [wdk, "AT"], [pk(pb)])
                    n = (h0 + t * TC) // TC
                    tt("vector", X[:, d, h0 + t * TC:h0 + (t + 1) * TC], X[:, d, h0 + t * TC:h0 + (t + 1) * TC],
                       PS[pb][:, :], ALU.add, [("X", d, n), pk(pb)], [("X", d, n)])

    for c in range(8):
        dma("sync", yT[c * 128:(c + 1) * 128, :], X[:, c, :], [("X", c, n) for n in range(4)], [("Y", c)])
    P.add("sync", lambda e: e.nop(), [("Y", c) for c in range(8)], ["FIN"])

    P.emit(nc, es)
    es.close()
    return nc


_CACHE = {}


def _rope_tables(core):
    pos0 = (core % 4) * NT
    t = np.arange(NT) + pos0
    row = (t // 64).astype(np.float32)
    col = (t % 64).astype(np.float32)
    inv = (1.0 / (10000.0 ** (np.arange(16, dtype=np.float32) * 2.0 / 32.0))).astype(np.float32)
    ang = np.zeros((128, NT), np.float32)
    for p in range(128):
        d = p % 64
        if d < 32:
            ang[p] = row * inv[d % 16]
        else:
            ang[p] = col * inv[(d - 32) % 16]
    return np.concatenate([np.cos(ang), np.sin(ang)], axis=1).astype(np.float32)


def _rotm():
    m = np.zeros((128, 128), np.float32)
    for hb in (0, 64):
        for base in (0, 32):
            for i in range(16):
                m[hb + base + 16 + i, hb + base + i] = -1.0
                m[hb + base + i, hb + base + 16 + i] = 1.0
    return m


def kernel(x, norm1_g, w_in, q_norm_g, k_norm_g, sg_norm_g, sg_w, sg_b, conv_w, w_out, norm2_g,
           ffn_w_up, ffn_conv_w, ffn_w_down, _n_layers=L_ALL, _stop=None):
    f = lambda a: np.ascontiguousarray(np.asarray(a, dtype=np.float32))
    x = f(x)
    L = L_ALL
    qcols = []
    for c in range(4):
        qcols += list(range(c * 64, c * 64 + 64)) + list(range((4 + c) * 64, (4 + c) * 64 + 64))
    perm = (qcols + list(range(512, 640)) + list(range(768, 1024)) + list(range(1280, 1536))
            + list(range(1536, 1792)) + list(range(1792, 2048)) + list(range(640, 768)) + list(range(1024, 1280)))
    w_in_p = f(np.asarray(w_in)[:, :, perm])
    w_out_f, w_up_f, w_down_f = f(w_out), f(ffn_w_up), f(ffn_w_down)
    n1, n2 = f(norm1_g), f(norm2_g)
    qg, kg, sgg, sgb, cw, fcw = f(q_norm_g), f(k_norm_g), f(sg_norm_g), f(sg_b), f(conv_w), f(ffn_conv_w)
    pp0 = np.zeros((128, NPP), np.float32)
    pp0[:, O_G1:O_G1 + 32] = n1.reshape(L, 8, 128).transpose(2, 0, 1).reshape(128, 32)
    pp0[:, O_G2:O_G2 + 32] = n2.reshape(L, 8, 128).transpose(2, 0, 1).reshape(128, 32)
    pidx = np.arange(128) % 64
    pp0[:, O_GQ:O_GQ + L] = qg[:, pidx].T
    pp0[:, O_GK:O_GK + L] = kg[:, pidx].T
    pp0[:, O_CW:O_CW + 24] = cw.reshape(L, 3, 2, 128).transpose(3, 0, 2, 1).reshape(128, 24)
    pp0[:, O_FCW:O_FCW + 528] = fcw.reshape(L, 3, 44, 128).transpose(3, 0, 2, 1).reshape(128, 528)
    qk = np.stack([qg, kg], axis=1).reshape(1, L * 128)
    pp0[:, O_QKG:O_QKG + L * 128] = np.broadcast_to(qk, (128, L * 128))
    pp0[:, NPP - 8:NPP] = sgg.reshape(L, 2, 128).transpose(2, 0, 1).reshape(128, 8)
    ppl = np.zeros((L, 128, 512), np.float32)
    for l in range(L):
        for i in range(2):
            for a in range(2):
                ppl[l, a * 64:(a + 1) * 64, i * 128:(i + 1) * 128] = sgb[l, 2 * i + a][None, :]
        ppl[l, :, 256:512] = sgg[l][None, :]
    sgwT = f(np.asarray(sg_w).transpose(0, 3, 1, 2).reshape(L, 128, 512))
    rot = _rotm()

    key = (_n_layers, _stop)
    if key not in _CACHE:
        _CACHE[key] = build_program(_n_layers, _stop)
    nc = _CACHE[key]

    in_maps = []
    for r in range(NCORE):
        b, s0 = r // 4, (r % 4) * NT
        pp_r = pp0.copy()
        if r % 4 > 0:
            pp_r[:, O_MASK + (r % 4) - 1] = 1.0
        if r % 4 < 3:
            pp_r[:, O_MASK + 4 + (r % 4) + 1] = 1.0
        in_maps.append({
            "xT": np.ascontiguousarray(x[b, s0:s0 + NT, :].T),
            "w_in": w_in_p, "w_out": w_out_f, "w_up": w_up_f, "w_down": w_down_f,
            "pp": pp_r, "ppl": ppl, "sgwT": sgwT, "rope": _rope_tables(r), "rotm": rot,
        })
    res = run_bass_kernel_spmd(nc, in_maps, core_ids=list(range(NCORE)))
    out = np.empty((2, 8192, D), np.float32)
    for r in range(NCORE):
        b, s0 = r // 4, (r % 4) * NT
        out[b, s0:s0 + NT, :] = np.asarray(res.results[r]["yT"]).T
    return out
```

```python
import numpy as np
from contextlib import ExitStack
import concourse.bass as bass
import concourse.mybir as mybir
from concourse.bass_utils import run_bass_kernel_spmd

F32 = mybir.dt.float32
BF16 = mybir.dt.bfloat16
AF = mybir.ActivationFunctionType
ALU = mybir.AluOpType
AX = mybir.AxisListType

L_ALL = 4
D = 1024
NT = 2048
TC = 512
NPROJ = 2048
DFF = 2816
NFF = 22
EPS = 1e-6
NCORE = 8
GROUPS = [[0, 1, 2, 3], [4, 5, 6, 7]]

O_G1 = 0
O_G2 = 32
O_GQ = 64
O_GK = 68
O_CW = 72
O_FCW = 96
O_MASK = 624
O_QKG = 632
NPP = O_QKG + L_ALL * 128 + 8

SEND1_W = 4144
V_OFF = 2048
E_OFF = 4128


class Op:
    __slots__ = ("eng", "fn", "deps", "kind", "sig", "sigval", "sem", "idx")

    def __init__(self, eng, fn, kind):
        self.eng = eng
        self.fn = fn
        self.kind = kind
        self.deps = []
        self.sig = kind != "c"
        self.sigval = None
        self.sem = None


class Prog:
    ENGS = ("sync", "scalar", "vector", "gpsimd", "tensor")

    def __init__(self):
        self.ops = {e: [] for e in self.ENGS}
        self.lw = {}
        self.lr = {}
        self.consts = set()

    def add(self, eng, fn, r=(), w=(), kind="c"):
        op = Op(eng, fn, kind)
        deps = set()
        for k in r:
            d = self.lw.get(k)
            if d is not None:
                deps.add(d)
        for k in w:
            d = self.lw.get(k)
            if d is not None:
                deps.add(d)
            deps.update(self.lr.get(k, ()))
        for k in w:
            self.lw[k] = op
            self.lr[k] = []
        for k in r:
            if k in self.consts:
                continue
            lst = self.lr.setdefault(k, [])
            if kind == "c":
                lst[:] = [o for o in lst if not (o.kind == "c" and o.eng == eng)]
            lst.append(op)
        for d in deps:
            if d is op:
                continue
            if d.kind == "c" and d.eng == "tensor" and eng == "tensor" and kind == "c":
                continue
            op.deps.append(d)
            d.sig = True
        self.ops[eng].append(op)
        return op

    def alias(self, new_keys, old_keys):
        acc = []
        for k in old_keys:
            d = self.lw.pop(k, None)
            if d is not None:
                acc.append(d)
            acc.extend(self.lr.pop(k, ()))
        for k in new_keys:
            self.lw.pop(k, None)
            self.lr[k] = list(acc)

    def emit(self, nc, es):
        R = 8
        sems = {}
        for e in ("scalar", "vector", "gpsimd", "tensor"):
            sems[e] = es.enter_context(nc.semaphore("c_" + e))
        pools = {e: [es.enter_context(nc.semaphore("d_%s%d" % (e, i))) for i in range(R)]
                 for e in ("sync", "gpsimd", "scalar")}
        CCR = 4
        ccsems = [es.enter_context(nc.semaphore("ccsem%d" % i)) for i in range(CCR)]
        cccount = 0
        for e in self.ENGS:
            cnt = 0
            dcnt = 0
            for op in self.ops[e]:
                if op.kind == "c":
                    if op.sig:
                        cnt += 1
                        op.sem = sems[e]
                        op.sigval = cnt
                elif op.kind == "d":
                    op.sem = pools[e][dcnt % R]
                    op.sigval = 16 * (dcnt // R + 1)
                    dcnt += 1
                else:
                    op.sem = ccsems[cccount % CCR]
                    op.sigval = cccount // CCR + 1
                    cccount += 1
        block = es.enter_context(nc.Block())
        prog = self

        def run(eng_name):
            def body(e):
                waited = {}
                for op in prog.ops[eng_name]:
                    need = {}
                    for d in op.deps:
                        key = id(d.sem)
                        if need.get(key, (None, 0))[1] < d.sigval:
                            need[key] = (d.sem, d.sigval)
                    if op.kind == "d" and op.sigval > 16:
                        key = id(op.sem)
                        if need.get(key, (None, 0))[1] < op.sigval - 16:
                            need[key] = (op.sem, op.sigval - 16)
                    if op.kind == "cc" and op.sigval > 1:
                        key = id(op.sem)
                        if need.get(key, (None, 0))[1] < op.sigval - 1:
                            need[key] = (op.sem, op.sigval - 1)
                    for key, (s, v) in need.items():
                        if waited.get(key, 0) >= v:
                            continue
                        e.wait_ge(s, v)
                        waited[key] = v
                    ins = op.fn(e)
                    if op.sig:
                        ins.then_inc(op.sem, 16 if op.kind == "d" else 1)
            return body

        block.sync(run("sync"))
        block.scalar(run("scalar"))
        block.vector(run("vector"))
        block.gpsimd(run("gpsimd"))
        block.tensor(run("tensor"))


def build_program(n_layers=L_ALL, stop=None):
    nc = bass.Bass("TRN2", target_bir_lowering=False)
    P = Prog()
    es = ExitStack()

    def din(name, shape, dt=F32):
        return nc.dram_tensor(name, list(shape), dt, kind="ExternalInput").ap()

    xT = din("xT", [D, NT])
    w_in = din("w_in", [L_ALL, D, NPROJ])
    w_out = din("w_out", [L_ALL, D, D])
    w_up = din("w_up", [L_ALL, D, 2 * DFF])
    w_down = din("w_down", [L_ALL, DFF, D])
    pp = din("pp", [128, NPP])
    ppl = din("ppl", [L_ALL, 128, 512])
    sgwT = din("sgwT", [L_ALL, 128, 512])
    rope = din("rope", [128, 2 * NT])
    rotm = din("rotm", [128, 128])
    yT = nc.dram_tensor("yT", [D, NT], F32, kind="ExternalOutput").ap()
    send1 = [nc.dram_tensor("send1_%d" % l, [128, NT], BF16).ap() for l in range(n_layers)]
    recv1 = [nc.dram_tensor("recv1_%d" % l, [4 * 128, NT], BF16).ap() for l in range(n_layers)]
    send1b = [nc.dram_tensor("send1b_%d" % l, [128, 2096], BF16).ap() for l in range(n_layers)]
    recv1b = [nc.dram_tensor("recv1b_%d" % l, [4 * 128, 2096], BF16).ap() for l in range(n_layers)]
    send2 = [nc.dram_tensor("send2_%d" % l, [128, 32], BF16).ap() for l in range(n_layers)]
    recv2 = [nc.dram_tensor("recv2_%d" % l, [4 * 128, 32], BF16).ap() for l in range(n_layers)]

    lscr = [nc.dram_tensor("lscr%d" % g, [1, 512], F32).ap() for g in range(2)]

    def sb(name, n, dt):
        return es.enter_context(nc.sbuf_tensor(name, [128, n], dt))

    X_t = sb("X", 8 * NT, F32)
    HB_t = sb("HB", 8192, BF16)
    AR_t = sb("AR", 22528, BF16)
    Q_t = sb("Q", 4 * NT, BF16)
    CB_t = sb("CB", 2 * NT, BF16)
    GO_t = sb("GO", 2 * NT, BF16)
    RC_t = sb("RC", 8192 + 32, BF16)
    PP_t = sb("PP", NPP, F32)
    PL_t = sb("PL", 512, F32)
    SGW_t = sb("SGW", 512, BF16)
    CST_t = sb("CST", 384, BF16)
    ONEF_t = sb("ONEF", 64, F32)
    BIAS_t = sb("BIAS", 4, F32)
    EPSB_t = sb("EPSB", 1, F32)
    TB_t = sb("TB", 4096, BF16)
    TF_t = sb("TF", 4608, F32)

    def V(t, off, *shape):
        n = int(np.prod(shape))
        a = t[:, off:off + n]
        if len(shape) == 1:
            return a
        if len(shape) == 2:
            return a.rearrange("p (a b) -> p a b", a=shape[0])
        if len(shape) == 3:
            return a.rearrange("p (a b c) -> p a b c", a=shape[0], b=shape[1])
        return a.rearrange("p (a b c d) -> p a b c d", a=shape[0], b=shape[1], c=shape[2])

    X = V(X_t, 0, 8, NT)
    Q = V(Q_t, 0, 4, NT)
    Qn = V(Q_t, 0, 16, 4, 128)
    CB = V(CB_t, 0, 2, NT)
    GO = V(GO_t, 0, 2, NT)
    ROPE = V(RC_t, 0, 2, NT)
    OT = V(RC_t, 0, 8, TC)
    CCX = V(RC_t, 4096, 2, NT + 2)
    EDG = V(RC_t, 8208, 4)
    EDG16 = V(RC_t, 8208, 16)
    ONES = CST_t[:, 0:128]
    BO = CST_t[:, 128:256]
    ROT = CST_t[:, 256:384]
    PPv = PP_t
    WI = V(AR_t, 0, 8, NPROJ)
    KO = V(AR_t, 16384, NT)
    VO = V(AR_t, 18432, 16, 2, 65)
    KA = V(AR_t, 0, 4, NT)
    VA = V(AR_t, 8192, 64, 2, 65)
    WOG = V(AR_t, 16512, 4, D)
    AT = V(AR_t, 0, NFF, 1024)
    WOA = V(HB_t, 0, 8, D)
    H = V(HB_t, 0, 2, 8, TC)
    H2 = V(HB_t, 0, 8, 1024)
    WU = V(Q_t, 0, 2, 2, 8, 256)
    WD = [V(CB_t, 0, NFF, 128), V(GO_t, 0, NFF, 128)]
    SU = V(TF_t, 3072, 2, 512)
    TSM = TF_t[:, 4104:4608]
    U = [V(TF_t, 0, 1026), V(TF_t, 1026, 1026)]
    CV = [V(TF_t, 2052, 1024), V(TF_t, 3076, 1024)]
    RCF = RC_t[:, 0:8208].bitcast(F32)
    U2 = [RCF[:, 0:1026], RCF[:, 1026:2052]]
    CV2 = [RCF[:, 2052:3076], RCF[:, 3076:4100]]
    RCK_F = [("U2", 0), ("U2", 1), ("CV2", 0), ("CV2", 1)]
    TFK = [("TF", i) for i in range(8)]
    UCK = [("U", 0), ("U", 1), ("CV", 0), ("CV", 1)]
    QCG = ["Q%d" % n for n in range(4)] + ["CBn%d" % n for n in range(4)] + ["GOn%d" % n for n in range(4)]
    WUD = [("WU", s_, g_) for s_ in range(2) for g_ in range(2)] + ["WD0", "WD1"]
    ARA = ["WI%d" % k for k in range(8)] + ["KO", "VO"]
    ARC = [("KA", r) for r in range(4)] + [("VA", r) for r in range(4)] + ["WOG"]

    def dma(eng, out, in_, r, w):
        return P.add(eng, lambda e: e.dma_start(out=out, in_=in_), r, w, kind="d")

    def mm(out, lhsT, rhs, start, stop, r, w):
        return P.add("tensor", lambda e: e.matmul(out, lhsT, rhs, start=start, stop=stop), r, w)

    def act(out, in_, func, r, w, bias=0.0, scale=1.0):
        return P.add("scalar", lambda e: e.activation(out=out, in_=in_, func=func, bias=bias, scale=scale), r, w)

    def tt(eng, out, a, b, op, r, w):
        return P.add(eng, lambda e: e.tensor_tensor(out=out, in0=a, in1=b, op=op), r, w)

    def ts(eng, out, a, s1, op0, r, w, s2=None, op1=None):
        if op1 is None:
            return P.add(eng, lambda e: e.tensor_scalar(out=out, in0=a, scalar1=s1, scalar2=None, op0=op0), r, w)
        return P.add(eng, lambda e: e.tensor_scalar(out=out, in0=a, scalar1=s1, scalar2=s2, op0=op0, op1=op1), r, w)

    def stt(out, a, s, b, op0, op1, r, w):
        return P.add("vector", lambda e: e.scalar_tensor_tensor(out=out, in0=a, scalar=s, in1=b, op0=op0, op1=op1), r, w)

    def recip(out, in_, r, w):
        return P.add("vector", lambda e: e.reciprocal(out=out, in_=in_), r, w)

    def cp(eng, out, in_, r, w):
        return P.add(eng, lambda e: e.tensor_copy(out=out, in_=in_), r, w)

    def memset(eng, ap, val, w):
        return P.add(eng, lambda e: e.memset(ap, val), (), w)

    def raw(eng, meth, r, w, kind="c", **kw):
        return P.add(eng, lambda e: getattr(e, meth)(**kw), r, w, kind=kind)

    PSALL = es.enter_context(nc.psum_tensor("psall", [128, 4096], F32))
    PS = [PSALL[:, i * 512:(i + 1) * 512] for i in range(8)]

    def pk(i):
        return ("ps", i)

    import os as _os
    for _ in range(int(_os.environ.get("KDUMMY", "0"))):
        P.add("sync", lambda e: e.nop(), (), ())
    dma("sync", PP_t[:, :], pp, (), ["PP"])
    dma("gpsimd", ROT, rotm, (), ["CST"])
    memset("vector", CST_t[:, 0:128], 1.0, ["CST1"])
    memset("vector", CST_t[:, 128:256], 0.0, ["CST2"])
    memset("vector", CST_t[0:64, 128:192], 1.0, ["CST2"])
    memset("vector", CST_t[64:128, 192:256], 1.0, ["CST2"])
    memset("vector", ONEF_t[:, :], 1.0, ["ONEF"])
    memset("vector", EPSB_t[:, :], EPS, ["EPSB"])
    memset("vector", TB_t[:, 2048:3072], 0.0, [("TB", 4), ("TB", 5)])
    memset("vector", TB_t[:, 3072:3584], 0.0, ["H2E"])
    memset("vector", EDG16, 0.0, ["EDG"])
    for c in range(8):
        dma("sync", X[:, c, :], xT[c * 128:(c + 1) * 128, :], (), [("X", c, n) for n in range(4)])
    P.consts.update(["PP", "CST", "CST1", "CST2", "ONEF", "EPSB"])

    tbn = [0]

    def tb():
        i = tbn[0] % 4
        tbn[0] += 1
        return TB_t[:, i * 512:(i + 1) * 512], ("TB", i)

    tfn = [0]

    def tf():
        i = tfn[0] % 6
        tfn[0] += 1
        return TF_t[:, i * 512:(i + 1) * 512], ("TF", i)

    def rms_sumsq(psb, src_fn, nchunk, lhsT, rkeys):
        for c in range(nchunk):
            sq, sqk = tb()
            src, sk = src_fn(c)
            act(sq, src, AF.Square, sk, [sqk])
            mm(PS[psb][:, :], lhsT, sq, c == 0, c == nchunk - 1, [sqk] + rkeys, [pk(psb)])

    def rstd_from(psb, inv_n, buf=None):
        rs, rk = buf if buf is not None else tf()
        act(rs, PS[psb][:, :], AF.Ln, [pk(psb), "EPSB"], [rk], bias=EPSB_t[:, 0:1], scale=inv_n)
        act(rs, rs, AF.Exp, [rk], [rk], scale=-0.5)
        return rs, rk

    def norm_chunk(l, goff, tok0, dst_fn, dstkeys, buf=None):
        n = tok0 // TC
        rms_sumsq(0, lambda c: (X[:, c, tok0:tok0 + TC], [("X", c, n)]), 8, ONES, ["CST1"])
        rs, rk = rstd_from(0, 1.0 / D, buf)
        for c in range(8):
            stt(dst_fn(c), X[:, c, tok0:tok0 + TC], PPv[:, goff + l * 8 + c:goff + l * 8 + c + 1], rs,
                ALU.mult, ALU.mult, [("X", c, n), rk, "PP"], dstkeys(c))

    for l in range(n_layers if stop != 'setup' else 0):
        P.alias(ARA, ["AT"] + ARC)
        P.alias([("H", 0), ("H", 1)], ["H2", "WOA"])
        P.alias(QCG, WUD)
        P.alias(TFK, UCK)
        P.alias([("ROPE", 0), ("ROPE", 1), "CCX"], ["OT", "CCX"] + RCK_F)
        for k in range(8):
            dma("gpsimd", WI[:, k, :], w_in[l, k * 128:(k + 1) * 128, :], (), ["WI%d" % k])
        dma("sync", PL_t[:, :], ppl[l], (), ["PL"])
        dma("gpsimd", SGW_t[:, :], sgwT[l], (), ["SGW"])
        dma("gpsimd", RC_t[:, 0:NT], rope[:, 0:NT], (), [("ROPE", 0)])
        dma("gpsimd", RC_t[:, NT:2 * NT], rope[:, NT:2 * NT], (), [("ROPE", 1)])
        RK = [("ROPE", 0), ("ROPE", 1)]
        VOf = AR_t[:, 18432:18432 + 16 * 130].rearrange("p (a c) -> p a c", c=65)
        memset("vector", VOf[:, :, 64:65], 1.0, ["VO"])
        qkg = PPv[:, O_QKG + l * 128:O_QKG + (l + 1) * 128].rearrange("p (a b) -> p a b", a=2)
        raw("vector", "tensor_reduce", ["PP"], ["TSM"], out=TSM[:, 0:2], in_=qkg, axis=AX.X, op=ALU.max,
            apply_absolute_value=True)
        ts("vector", BIAS_t[:, l:l + 1], TSM[:, 0:1], TSM[:, 1:2], ALU.mult, ["TSM"], [("BIAS", l)], s2=-8.0,
           op1=ALU.mult)

        O_GV = NPP - 8

        def send_block():
            for i in range(2):
                cp("vector", EDG[:, 2 * i:2 * i + 1], CCX[:, i, 1:2], ["CCX"], ["EDG"])
                cp("vector", EDG[:, 2 * i + 1:2 * i + 2], CCX[:, i, NT:NT + 1], ["CCX"], ["EDG"])
            dma("sync", send1[l], KO, ["KO"], [("S1", l, 0)])
            dma("sync", send1b[l][:, 0:2080], AR_t[:, 18432:18432 + 2080], ["VO"], [("S1", l, 1)])
            dma("sync", send1b[l][:, 2080:2096], EDG16, ["EDG"], [("S1", l, 2)])
            P.add("gpsimd", lambda e, l=l: e.collective_compute("AllGather", ALU.bypass, replica_groups=GROUPS,
                                                                ins=[send1[l]], outs=[recv1[l]]),
                  [("S1", l, 0)], [("R1", l)], kind="cc")
            P.add("gpsimd", lambda e, l=l: e.collective_compute("AllGather", ALU.bypass, replica_groups=GROUPS,
                                                                ins=[send1b[l]], outs=[recv1b[l]]),
                  [("S1", l, 1), ("S1", l, 2)], [("R1b", l)], kind="cc")

        for n in range(4):
            tok0 = n * TC
            hb = n % 2
            hk = ("H", hb)
            if n == 0:
                norm_chunk(l, O_G1, tok0, lambda c: H[:, hb, c, :], lambda c: [hk])
            cct = [None, None]
            st = {}
            pbc = [0]

            def stage0(j):
                pbc[0] += 1
                pb = 1 + (pbc[0] % 2)
                for k in range(8):
                    mm(PS[pb][:, :], WI[:, k, j * 128:(j + 1) * 128], H[:, hb, k, :], k == 0, k == 7,
                       [hk, "WI%d" % k], [pk(pb)])
                if j <= 4:
                    qf, qfk = tf()
                    act(qf, PS[pb][:, :], AF.Copy, [pk(pb)], [qfk])
                    sq, sqk = tb()
                    act(sq, qf, AF.Square, [qfk], [sqk])
                    st[j] = dict(qf=qf, qfk=qfk, sq=sq, sqk=sqk)
                elif j <= 6:
                    act(SU[:, j - 5, :], PS[pb][:, :], AF.Copy, [pk(pb)], [("TF", 6 + j - 5)])
                elif j <= 8:
                    act(CB[:, j - 7, tok0:tok0 + TC], PS[pb][:, :], AF.Copy, [pk(pb)], ["CBn%d" % n])
                elif j <= 10:
                    c_, ck_ = tf()
                    act(c_, PS[pb][:, :], AF.Copy, [pk(pb)], [ck_])
                    cct[j - 9] = (c_, ck_)
                else:
                    i = j - 11
                    c_, ck_ = cct[i]
                    tt("vector", CCX[:, i, 1 + tok0:1 + tok0 + TC], PS[pb][:, :], c_, ALU.mult,
                       [pk(pb), ck_], ["CCX"])

            def stage1(j):
                d_ = st[j]
                gcol = (O_GQ if j < 4 else O_GK) + l
                mm(PS[3][:, :], BO, d_["sq"], True, True, [d_["sqk"], "CST2"], [pk(3)])
                rs, rk = rstd_from(3, 1.0 / 64)
                qn, qnk = tb()
                stt(qn, d_["qf"], PPv[:, gcol:gcol + 1], rs, ALU.mult, ALU.mult, [d_["qfk"], rk, "PP"], [qnk])
                d_["qn"], d_["qnk"] = qn, qnk

            def stage2(j):
                d_ = st[j]
                qn, qnk = d_["qn"], d_["qnk"]
                mm(PS[4][:, :], ROT, qn, True, True, [qnk, "CST"], [pk(4)])
                t1, t1k = tf()
                tt("gpsimd", t1, qn, ROPE[:, 0, tok0:tok0 + TC], ALU.mult, [qnk] + RK, [t1k])
                t2, t2k = tf()
                tt("vector", t2, PS[4][:, :], ROPE[:, 1, tok0:tok0 + TC], ALU.mult, [pk(4)] + RK, [t2k])
                if j < 4:
                    tt("vector", Qn[:, n * 4:(n + 1) * 4, j, :], t1.rearrange("p (b q) -> p b q", b=4),
                       t2.rearrange("p (b q) -> p b q", b=4), ALU.add, [t1k, t2k], ["Q%d" % n])
                else:
                    tt("vector", KO[:, tok0:tok0 + TC], t1, t2, ALU.add, [t1k, t2k], ["KO"])

            def run_steps(order):
                L_ = len(order)
                for s_ in range(L_ + 2):
                    if s_ < L_:
                        stage0(order[s_])
                    if 0 <= s_ - 1 < L_ and order[s_ - 1] <= 4:
                        stage1(order[s_ - 1])
                    if 0 <= s_ - 2 < L_ and order[s_ - 2] <= 4:
                        stage2(order[s_ - 2])

            run_steps([4, 9, 10, 11, 12])
            def tm_a(t4):
                tile = n * 4 + t4
                tmb = 5 if tile % 2 == 0 else 0
                for k in range(8):
                    mm(PS[tmb][:, 0:384], H[:, hb, k, t4 * 128:(t4 + 1) * 128], WI[:, k, 1664:2048], k == 0, k == 7,
                       [hk, "WI%d" % k], [pk(tmb)])
                sq, sqk = tf()
                act(sq[:, 0:256], PS[tmb][:, 128:384], AF.Square, [pk(tmb)], [sqk])
                act(VO[:, tile, :, 0:64], PS[tmb][:, 0:128].rearrange("p (g c) -> p g c", g=2), AF.Copy,
                    [pk(tmb)], ["VO"])
                ssk = ("SS", t4 % 2)
                ss = TSM[:, 8 + (t4 % 2) * 4: 9 + (t4 % 2) * 4]
                raw("vector", "tensor_reduce", [sqk], [ssk], out=ss, in_=sq[:, 0:256], axis=AX.X, op=ALU.add)
                act(ss, ss, AF.Ln, [ssk, "EPSB"], [ssk], bias=EPSB_t[:, 0:1], scale=1.0 / 256)
                act(ss, ss, AF.Exp, [ssk], [ssk], scale=-0.5)
                vb = tile % 2
                vnp = V(TB_t, 2048 + vb * 512, 2, 2, 128)
                vnk = ("TB", 4 + vb)
                for a in range(2):
                    ts("vector", vnp[:, :, a, a * 64:(a + 1) * 64],
                       PS[tmb][:, 128:384].rearrange("p (i a c) -> p i a c", i=2, a=2)[:, :, a, :], ss,
                       ALU.mult, [pk(tmb), ssk], [vnk])

            def tm_b(t4):
                tile = n * 4 + t4
                vb = tile % 2
                vnp = V(TB_t, 2048 + vb * 512, 2, 2, 128)
                vnk = ("TB", 4 + vb)
                for i in range(2):
                    for a in range(2):
                        mm(PS[6 + i][:, t4 * 128:(t4 + 1) * 128], vnp[:, i, a, :],
                           SGW_t[:, (2 * i + a) * 128:(2 * i + a + 1) * 128], a == 0, a == 1,
                           [vnk, "SGW"], [pk(6 + i)])

            for t4 in range(5):
                if t4 < 4:
                    tm_a(t4)
                if t4 >= 1:
                    tm_b(t4 - 1)
            if n == 3:
                send_block()
            else:
                nb = (n + 1) % 2
                norm_chunk(l, O_G1, (n + 1) * TC, lambda c, nb=nb: H[:, nb, c, :], lambda c, nb=nb: [("H", nb)])
            run_steps([0, 1, 2, 3, 5, 6, 7, 8])
            for i in range(2):
                t_, tk_ = tf()
                b2 = PL_t[:, i * 128:(i + 1) * 128]
                gvs = PPv[:, O_GV + l * 2 + i:O_GV + l * 2 + i + 1]
                for t4 in range(4):
                    stt(t_[:, t4 * 128:(t4 + 1) * 128], PS[6 + i][:, t4 * 128:(t4 + 1) * 128], gvs, b2,
                        ALU.mult, ALU.add, [pk(6 + i), "PL", "PP"], [tk_])
                tt("vector", GO[:, i, tok0:tok0 + TC], t_, SU[:, i, :], ALU.mult, [tk_, ("TF", 6 + i)],
                   ["GOn%d" % n])

        if stop == 'A':
            break
        P.alias(ARC, ARA)
        r1 = recv1[l].rearrange("(r p) c -> p r c", p=128)
        r1b = recv1b[l].rearrange("(r p) c -> p r c", p=128)
        for r in range(4):
            dma("sync", KA[:, r, :], r1[:, r, :], [("R1", l)], [("KA", r)])
            dma("scalar", AR_t[:, 8192 + r * 2080: 8192 + (r + 1) * 2080], r1b[:, r, 0:2080],
                [("R1b", l)], [("VA", r)])
        E4 = V(TB_t, 3072, 4, 4)
        dma("sync", E4, r1b[:, :, 2080:2084], [("R1b", l)], ["E4"])
        if stop == 'B2':
            break
        P.alias(["WOA"], [("H", 0), ("H", 1)])
        wo = w_out[l]
        dma("gpsimd", WOA[0:64, :, :], wo[0:512, :].rearrange("(h p) d -> p h d", p=64), (), ["WOA"])
        dma("gpsimd", WOG, wo[512:1024, :].rearrange("(c p) d -> p c d", p=128), (), ["WOG"])
        if stop == 'B3':
            break
        hal = TSM[:, 16:20]
        for side in range(2):
            e_idx = 1 if side == 0 else 0

            def src(r, e_idx=e_idx):
                return E4[:, r, :].rearrange("p (i e) -> p i e", i=2)[:, :, e_idx]
            dst = hal[:, side * 2:side * 2 + 2]
            mcol = O_MASK + side * 4
            ts("vector", dst, src(0), PPv[:, mcol:mcol + 1], ALU.mult, ["E4", "PP"], ["HAL"])
            for r in range(1, 4):
                stt(dst, src(r), PPv[:, mcol + r:mcol + r + 1], dst, ALU.mult, ALU.add, ["E4", "PP", "HAL"], ["HAL"])
        for i in range(2):
            cp("vector", CCX[:, i, 0:1], hal[:, i:i + 1], ["HAL"], ["CCX"])
            cp("vector", CCX[:, i, NT + 1:NT + 2], hal[:, 2 + i:3 + i], ["HAL"], ["CCX"])
        for n in range(4):
            tok0 = n * TC
            for i in range(2):
                cw = O_CW + l * 6 + i * 3
                t_, tk_ = tf()
                ts("vector", t_, CCX[:, i, tok0:tok0 + TC], PPv[:, cw:cw + 1], ALU.mult, ["CCX", "PP"], [tk_])
                stt(t_, CCX[:, i, tok0 + 1:tok0 + 1 + TC], PPv[:, cw + 1:cw + 2], t_, ALU.mult, ALU.add,
                    ["CCX", "PP", tk_], [tk_])
                stt(t_, CCX[:, i, tok0 + 2:tok0 + 2 + TC], PPv[:, cw + 2:cw + 3], t_, ALU.mult, ALU.add,
                    ["CCX", "PP", tk_], [tk_])
                tt("vector", CB[:, i, tok0:tok0 + TC], CB[:, i, tok0:tok0 + TC], t_, ALU.mult,
                   ["CBn%d" % n, tk_], ["CBn%d" % n])

        if stop == 'B':
            break
        P.alias(["OT"], RK)

        ecols = [0, 1023, 1024, 2047]

        def edge_norm(es, slot, psb):
            o = 128 + slot * 96
            XE = TSM[:, o:o + 32].rearrange("p (c e) -> p c e", c=8)
            xg = TSM[:, o + 32:o + 64].rearrange("p (c e) -> p c e", c=8)
            rse = TSM[:, o + 64:o + 68]
            sqe = TB_t[:, 3312 + slot * 32:3344 + slot * 32]
            H2E_ = TB_t[:, 3120:3152].rearrange("p (c e) -> p c e", c=8)
            kx, ks, kr, kg = ("XE", slot), ("SQE", slot), ("RSE", slot), ("XG", slot)
            memset("vector", TSM[:, o:o + 32], 1.0, [kx])
            for e_i in es:
                ec = ecols[e_i]
                cp("vector", XE[:, :, e_i], X[:, :, ec], [("X", c, ec // TC) for c in range(8)], [kx])
            act(sqe, TSM[:, o:o + 32], AF.Square, [kx], [ks])
            sqe3 = sqe.rearrange("p (c e) -> p c e", c=8)
            for c in range(8):
                mm(PS[psb][:, 0:4], ONES, sqe3[:, c, :], c == 0, c == 7, [ks, "CST1"], [pk(psb)])
            act(rse, PS[psb][:, 0:4], AF.Ln, [pk(psb), "EPSB"], [kr], bias=EPSB_t[:, 0:1], scale=1.0 / D)
            act(rse, rse, AF.Exp, [kr], [kr], scale=-0.5)
            g2 = PPv[:, O_G2 + l * 8:O_G2 + l * 8 + 8]
            for e_i in es:
                tt("vector", xg[:, :, e_i], XE[:, :, e_i], g2, ALU.mult, [kx, "PP"], [kg])
                ts("vector", H2E_[:, :, e_i], xg[:, :, e_i], rse[:, e_i:e_i + 1], ALU.mult, [kg, kr], ["H2E"])

        def edge_exchange():
            edge_norm([0, 3], 0, 6)
            dma("sync", send2[l], TB_t[:, 3120:3152], ["H2E"], [("S2", l)])
            P.add("gpsimd", lambda e, l=l: e.collective_compute("AllGather", ALU.bypass, replica_groups=GROUPS,
                                                                ins=[send2[l]], outs=[recv2[l]]),
                  [("S2", l)], [("R2", l)], kind="cc")
            dma("sync", TB_t[:, 3152:3280].rearrange("p (r c) -> p r c", r=4),
                recv2[l].rearrange("(r p) c -> p r c", p=128), [("R2", l)], ["E2"])

        pend = []

        def wout_closures(n, d):
            tok0 = n * TC
            pb = 7
            cl = []
            for h in range(8):
                cl.append(lambda h=h: mm(PS[pb][:, :], WOA[0:64, h, d * 128:(d + 1) * 128], OT[0:64, h, :],
                                         h == 0, False, ["WOA", "OT"], [pk(pb)]))
            for i in range(2):
                cl.append(lambda i=i: mm(PS[pb][:, :], WOG[:, i, d * 128:(d + 1) * 128], GO[:, i, tok0:tok0 + TC],
                                         False, False, ["WOG", "GOn%d" % n], [pk(pb)]))

            def last(i):
                mm(PS[pb][:, :], WOG[:, 2 + i, d * 128:(d + 1) * 128], CB[:, i, tok0:tok0 + TC], False, i == 1,
                   ["WOG", "CBn%d" % n], [pk(pb)])
                if i == 1:
                    tt("vector", X[:, d, tok0:tok0 + TC], X[:, d, tok0:tok0 + TC], PS[pb][:, :], ALU.add,
                       [("X", d, n), pk(pb)], [("X", d, n)])
            for i in range(2):
                cl.append(lambda i=i: last(i))
            return cl

        LB = V(TF_t, 1024, 2, 512)

        def evac(n, qb):
            for g in range(2):
                os_ = TF_t[:, g * 512:(g + 1) * 512]
                cp("vector", os_[0:65, :], PS[4 + g][0:65, :], [pk(4 + g)], [("TF", g)])
            for g in range(2):
                os_ = TF_t[:, g * 512:(g + 1) * 512]
                osk = ("TF", g)
                lbk = ("TF", 2 + g)
                dma("sync", lscr[g], os_[64:65, :], [osk], [("LD", g)])
                dma("sync", LB[0:64, g, :], lscr[g].partition_broadcast(64), [("LD", g)], [lbk])
                recip(LB[0:64, g, :], LB[0:64, g, :], [lbk], [lbk])
                tt("vector", OT[0:64, 4 * g:4 * g + 4, qb * 128:(qb + 1) * 128],
                   os_[0:64, :].rearrange("p (h q) -> p h q", h=4),
                   LB[0:64, g, :].rearrange("p (h q) -> p h q", h=4), ALU.mult, [osk, lbk], ["OT"])

        items = [(n, qb, kt) for n in (0, 3, 1, 2) for qb in range(4) for kt in range(64)]

        def emit_qk(it):
            n, qb, kt = items[it]
            q0 = n * TC + qb * 128
            sp = it % 2
            for g in range(2):
                mm(PS[sp * 2 + g][:, :].rearrange("p (h q) -> p h q", h=4),
                   KA[64 * g:64 * g + 64, kt // 16, (kt % 16) * 128:(kt % 16 + 1) * 128],
                   Qn[64 * g:64 * g + 64, n * 4 + qb, :, :], True, True, [("KA", kt // 16), "Q%d" % n],
                   [("sp", sp)])

        PTB = [TB_t[:, 0:1024], TB_t[:, 1024:2048], TF_t[:, 2048:2560].bitcast(BF16)]
        PTK = [[("TB", 0), ("TB", 1)], [("TB", 2), ("TB", 3)], [("TF", 4)]]

        def emit_exp(it):
            sp = it % 2
            pb_ = it % 3
            act(PTB[pb_], PSALL[:, sp * 1024:(sp + 1) * 1024], AF.Exp, [("sp", sp)], PTK[pb_], scale=0.125)

        def emit_pv(it):
            n, qb, kt = items[it]
            pb_ = it % 3
            for g in range(2):
                mm(PS[4 + g][0:65, :], VA[:, kt, g, :], PTB[pb_][:, g * 512:(g + 1) * 512],
                   kt == 0, kt == 63, [("VA", kt // 16)] + PTK[pb_], [pk(4 + g)])
            if kt == 63:
                while pend:
                    pend.pop(0)()
                evac(n, qb)
                if qb == 3:
                    for d in range(8):
                        pend.extend(wout_closures(n, d))
                    if n == 3:
                        pend.append(edge_exchange)

        P.alias([("sp", 0), ("sp", 1)], [pk(0), pk(1), pk(2), pk(3)])
        nit = len(items)
        emit_qk(0)
        emit_qk(1)
        emit_exp(0)
        emit_exp(1)
        for it in range(nit):
            if it + 2 < nit:
                emit_qk(it + 2)
            emit_pv(it)
            if it + 2 < nit:
                emit_exp(it + 2)
            kt = items[it][2]
            if 14 <= kt < 63:
                for _ in range(2):
                    if pend:
                        pend.pop(0)()
        while pend:
            pend.pop(0)()
        P.alias([pk(0), pk(1), pk(2), pk(3)], [("sp", 0), ("sp", 1)])

        if stop == 'C':
            break
        P.alias(["H2"], ["WOA"])
        P.alias(["AT"], ARC)
        P.alias(WUD, QCG)
        P.alias(UCK, TFK)
        P.alias(RCK_F, ["OT", "CCX"])
        wu_list = [(hf, b) for hf in range(2) for b in range(11)]
        wd_list = [(hf, d) for hf in range(2) for d in range(8)]
        wu_next = [0]
        wd_next = [0]

        def issue_wu(upto):
            while wu_next[0] <= upto and wu_next[0] < len(wu_list):
                q_ = wu_next[0]
                hf_, b_ = wu_list[q_]
                st_ = q_ % 2
                for gv in range(2):
                    col0 = gv * DFF + b_ * 256
                    dma("gpsimd", WU[:, st_, gv, :, :],
                        w_up[l, :, col0:col0 + 256].rearrange("(k p) n -> p k n", p=128), (), [("WU", st_, gv)])
                wu_next[0] += 1

        def issue_wd(upto):
            while wd_next[0] <= upto and wd_next[0] < len(wd_list):
                q_ = wd_next[0]
                hf_, d_ = wd_list[q_]
                st_ = q_ % 2
                dma("gpsimd", WD[st_], w_down[l, :, d_ * 128:(d_ + 1) * 128].rearrange("(j p) n -> p j n", p=128),
                    (), ["WD%d" % st_])
                wd_next[0] += 1

        issue_wu(1)
        issue_wd(1)
        edge_norm([1, 2], 1, 0)
        E2 = TB_t[:, 3152:3280].rearrange("p (r c e) -> p r c e", r=4, c=8)
        H2E = TB_t[:, 3120:3152].rearrange("p (c e) -> p c e", c=8)
        halh = TSM[:, 96:112].rearrange("p (s c) -> p s c", s=2)
        for side in range(2):
            e_idx = 3 if side == 0 else 0
            mcol = O_MASK + side * 4
            ts("vector", halh[:, side, :], E2[:, 0, :, e_idx], PPv[:, mcol:mcol + 1], ALU.mult, ["E2", "PP"], ["HALH"])
            for r in range(1, 4):
                stt(halh[:, side, :], E2[:, r, :, e_idx], PPv[:, mcol + r:mcol + r + 1], halh[:, side, :],
                    ALU.mult, ALU.add, ["E2", "PP", "HALH"], ["HALH"])
        HH = TB_t[:, 3280:3312].rearrange("p (f k e) -> p f k e", f=2, k=8)
        cp("vector", HH[:, 0, :, 0], halh[:, 0, :], ["HALH"], ["HH"])
        cp("vector", HH[:, 0, :, 1], H2E[:, :, 2], ["H2E"], ["HH"])
        cp("vector", HH[:, 1, :, 0], H2E[:, :, 1], ["H2E"], ["HH"])
        cp("vector", HH[:, 1, :, 1], halh[:, 1, :], ["HALH"], ["HH"])

        epi_pend = []
        for hf in range(2):
            h0 = hf * 1024
            for t in range(2):
                norm_chunk(l, O_G2, h0 + t * TC, lambda c, t=t: H2[:, c, t * TC:(t + 1) * TC], lambda c: ["H2"],
                           buf=(CV[1][:, 0:512], ("CV", 1)))
            for b in range(11):
                q_ = hf * 11 + b
                st = q_ % 2
                issue_wu(q_ + 1)
                for jj in range(2):
                    j = 2 * b + jj
                    for gv in range(2):
                        wuk = ("WU", st, gv)
                        for t in range(2):
                            pb = 1 + gv * 2 + t
                            for k in range(8):
                                mm(PS[pb][:, :], WU[:, st, gv, k, jj * 128:(jj + 1) * 128],
                                   H2[:, k, t * TC:(t + 1) * TC], k == 0, k == 7, [wuk, "H2"], [pk(pb)])
                        hbk = 5 if j % 2 == 0 else 0
                        for k in range(8):
                            mm(PS[hbk][:, gv * 2:gv * 2 + 2], WU[:, st, gv, k, jj * 128:(jj + 1) * 128],
                               HH[:, hf, k, :], k == 0, k == 7, [wuk, "HH"], [pk(hbk)])
                    bs = j % 2
                    Ub, CVb = (U, CV) if bs == 0 else (U2, CV2)
                    for gv in range(2):
                        uk = ("U", gv) if bs == 0 else ("U2", gv)
                        act(Ub[gv][:, 1:513], PS[1 + gv * 2][:, :], AF.Copy, [pk(1 + gv * 2)], [uk])
                        act(Ub[gv][:, 513:1025], PS[2 + gv * 2][:, :], AF.Copy, [pk(2 + gv * 2)], [uk])
                        hbk = 5 if j % 2 == 0 else 0
                        cp("vector", Ub[gv][:, 0:1], PS[hbk][:, gv * 2:gv * 2 + 1], [pk(hbk)], [uk])
                        cp("vector", Ub[gv][:, 1025:1026], PS[hbk][:, gv * 2 + 1:gv * 2 + 2], [pk(hbk)], [uk])
                        fw = O_FCW + (l * 44 + gv * 22 + j) * 3
                        ck = ("CV", gv) if bs == 0 else ("CV2", gv)
                        act(CVb[gv], Ub[gv][:, 0:1024], AF.Copy, [uk, "PP"], [ck], scale=PPv[:, fw:fw + 1])
                        stt(CVb[gv], Ub[gv][:, 1:1025], PPv[:, fw + 1:fw + 2], CVb[gv], ALU.mult, ALU.add,
                            [uk, "PP", ck], [ck])
                        stt(CVb[gv], Ub[gv][:, 2:1026], PPv[:, fw + 2:fw + 3], CVb[gv], ALU.mult, ALU.add,
                            [uk, "PP", ck], [ck])
                    ck0 = ("CV", 0) if bs == 0 else ("CV2", 0)
                    ck1 = ("CV", 1) if bs == 0 else ("CV2", 1)

                    def epi(CVb=CVb, ck0=ck0, ck1=ck1, j=j):
                        act(CVb[0], CVb[0], AF.Silu, [ck0], [ck0])
                        tt("vector", AT[:, j, :], CVb[0], CVb[1], ALU.mult, [ck0, ck1], ["AT"])
                    if epi_pend:
                        epi_pend.pop(0)()
                    epi_pend.append(epi)
            while epi_pend:
                epi_pend.pop(0)()
            for d in range(8):
                q_ = hf * 8 + d
                st = q_ % 2
                wdk = "WD%d" % st
                issue_wd(q_ + 1)
                for t in range(2):
                    pb = 6 + t
                    for j in range(NFF):
                        mm(PS[pb][:, :], WD[st][:, j, :], AT[:, j, t * TC:(t + 1) * TC], j == 0, j == NFF - 1,
                           [wdk, "AT"], [pk(pb)])
                    n = (h0 + t * TC) // TC
                    tt("vector", X[:, d, h0 + t * TC:h0 + (t + 1) * TC], X[:, d, h0 + t * TC:h0 + (t + 1) * TC],
                       PS[pb][:, :], ALU.add, [("X", d, n), pk(pb)], [("X", d, n)])

    for c in range(8):
        dma("sync", yT[c * 128:(c + 1) * 128, :], X[:, c, :], [("X", c, n) for n in range(4)], [("Y", c)])
    P.add("sync", lambda e: e.nop(), [("Y", c) for c in range(8)], ["FIN"])

    P.emit(nc, es)
    es.close()
    return nc


_CACHE = {}


def _rope_tables(core):
    pos0 = (core % 4) * NT
    t = np.arange(NT) + pos0
    row = (t // 64).astype(np.float32)
    col = (t % 64).astype(np.float32)
    inv = (1.0 / (10000.0 ** (np.arange(16, dtype=np.float32) * 2.0 / 32.0))).astype(np.float32)
    ang = np.zeros((128, NT), np.float32)
    for p in range(128):
        d = p % 64
        if d < 32:
            ang[p] = row * inv[d % 16]
        else:
            ang[p] = col * inv[(d - 32) % 16]
    return np.concatenate([np.cos(ang), np.sin(ang)], axis=1).astype(np.float32)


def _rotm():
    m = np.zeros((128, 128), np.float32)
    for hb in (0, 64):
        for base in (0, 32):
            for i in range(16):
                m[hb + base + 16 + i, hb + base + i] = -1.0
                m[hb + base + i, hb + base + 16 + i] = 1.0
    return m


def kernel(x, norm1_g, w_in, q_norm_g, k_norm_g, sg_norm_g, sg_w, sg_b, conv_w, w_out, norm2_g,
           ffn_w_up, ffn_conv_w, ffn_w_down, _n_layers=L_ALL, _stop=None):
    f = lambda a: np.ascontiguousarray(np.asarray(a, dtype=np.float32))
    x = f(x)
    L = L_ALL
    qcols = []
    for c in range(4):
        qcols += list(range(c * 64, c * 64 + 64)) + list(range((4 + c) * 64, (4 + c) * 64 + 64))
    perm = (qcols + list(range(512, 640)) + list(range(768, 1024)) + list(range(1280, 1536))
            + list(range(1536, 1792)) + list(range(1792, 2048)) + list(range(640, 768)) + list(range(1024, 1280)))
    w_in_p = f(np.asarray(w_in)[:, :, perm])
    w_out_f, w_up_f, w_down_f = f(w_out), f(ffn_w_up), f(ffn_w_down)
    n1, n2 = f(norm1_g), f(norm2_g)
    qg, kg, sgg, sgb, cw, fcw = f(q_norm_g), f(k_norm_g), f(sg_norm_g), f(sg_b), f(conv_w), f(ffn_conv_w)
    pp0 = np.zeros((128, NPP), np.float32)
    pp0[:, O_G1:O_G1 + 32] = n1.reshape(L, 8, 128).transpose(2, 0, 1).reshape(128, 32)
    pp0[:, O_G2:O_G2 + 32] = n2.reshape(L, 8, 128).transpose(2, 0, 1).reshape(128, 32)
    pidx = np.arange(128) % 64
    pp0[:, O_GQ:O_GQ + L] = qg[:, pidx].T
    pp0[:, O_GK:O_GK + L] = kg[:, pidx].T
    pp0[:, O_CW:O_CW + 24] = cw.reshape(L, 3, 2, 128).transpose(3, 0, 2, 1).reshape(128, 24)
    pp0[:, O_FCW:O_FCW + 528] = fcw.reshape(L, 3, 44, 128).transpose(3, 0, 2, 1).reshape(128, 528)
    qk = np.stack([qg, kg], axis=1).reshape(1, L * 128)
    pp0[:, O_QKG:O_QKG + L * 128] = np.broadcast_to(qk, (128, L * 128))
    pp0[:, NPP - 8:NPP] = sgg.reshape(L, 2, 128).transpose(2, 0, 1).reshape(128, 8)
    ppl = np.zeros((L, 128, 512), np.float32)
    for l in range(L):
        for i in range(2):
            for a in range(2):
                ppl[l, a * 64:(a + 1) * 64, i * 128:(i + 1) * 128] = sgb[l, 2 * i + a][None, :]
        ppl[l, :, 256:512] = sgg[l][None, :]
    sgwT = f(np.asarray(sg_w).transpose(0, 3, 1, 2).reshape(L, 128, 512))
    rot = _rotm()

    key = (_n_layers, _stop)
    if key not in _CACHE:
        _CACHE[key] = build_program(_n_layers, _stop)
    nc = _CACHE[key]

    in_maps = []
    for r in range(NCORE):
        b, s0 = r // 4, (r % 4) * NT
        pp_r = pp0.copy()
        if r % 4 > 0:
            pp_r[:, O_MASK + (r % 4) - 1] = 1.0
        if r % 4 < 3:
            pp_r[:, O_MASK + 4 + (r % 4) + 1] = 1.0
        in_maps.append({
            "xT": np.ascontiguousarray(x[b, s0:s0 + NT, :].T),
            "w_in": w_in_p, "w_out": w_out_f, "w_up": w_up_f, "w_down": w_down_f,
            "pp": pp_r, "ppl": ppl, "sgwT": sgwT, "rope": _rope_tables(r), "rotm": rot,
        })
    res = run_bass_kernel_spmd(nc, in_maps, core_ids=list(range(NCORE)))
    out = np.empty((2, 8192, D), np.float32)
    for r in range(NCORE):
        b, s0 = r // 4, (r % 4) * NT
        out[b, s0:s0 + NT, :] = np.asarray(res.results[r]["yT"]).T
    return out
```

```python
import numpy as np
from contextlib import ExitStack
import concourse.bass as bass
import concourse.mybir as mybir
from concourse.bass_utils import run_bass_kernel_spmd

F32 = mybir.dt.float32
BF16 = mybir.dt.bfloat16
AF = mybir.ActivationFunctionType
ALU = mybir.AluOpType
AX = mybir.AxisListType

L_ALL = 4
D = 1024
NT = 2048
TC = 512
NPROJ = 2048
DFF = 2816
NFF = 22
EPS = 1e-6
NCORE = 8
GROUPS = [[0, 1, 2, 3], [4, 5, 6, 7]]

O_G1 = 0
O_G2 = 32
O_GQ = 64
O_GK = 68
O_CW = 72
O_FCW = 96
O_MASK = 624
O_QKG = 632
NPP = O_QKG + L_ALL * 128 + 8

SEND1_W = 4144
V_OFF = 2048
E_OFF = 4128


class Op:
    __slots__ = ("eng", "fn", "deps", "kind", "sig", "sigval", "sem", "idx")

    def __init__(self, eng, fn, kind):
        self.eng = eng
        self.fn = fn
        self.kind = kind
        self.deps = []
        self.sig = kind != "c"
        self.sigval = None
        self.sem = None


class Prog:
    ENGS = ("sync", "scalar", "vector", "gpsimd", "tensor")

    def __init__(self):
        self.ops = {e: [] for e in self.ENGS}
        self.lw = {}
        self.lr = {}
        self.consts = set()

    def add(self, eng, fn, r=(), w=(), kind="c"):
        op = Op(eng, fn, kind)
        deps = set()
        for k in r:
            d = self.lw.get(k)
            if d is not None:
                deps.add(d)
        for k in w:
            d = self.lw.get(k)
            if d is not None:
                deps.add(d)
            deps.update(self.lr.get(k, ()))
        for k in w:
            self.lw[k] = op
            self.lr[k] = []
        for k in r:
            if k in self.consts:
                continue
            lst = self.lr.setdefault(k, [])
            if kind == "c":
                lst[:] = [o for o in lst if not (o.kind == "c" and o.eng == eng)]
            lst.append(op)
        for d in deps:
            if d is op:
                continue
            if d.kind == "c" and d.eng == "tensor" and eng == "tensor" and kind == "c":
                continue
            op.deps.append(d)
            d.sig = True
        self.ops[eng].append(op)
        return op

    def alias(self, new_keys, old_keys):
        acc = []
        for k in old_keys:
            d = self.lw.pop(k, None)
            if d is not None:
                acc.append(d)
            acc.extend(self.lr.pop(k, ()))
        for k in new_keys:
            self.lw.pop(k, None)
            self.lr[k] = list(acc)

    def emit(self, nc, es):
        R = 8
        sems = {}
        for e in ("scalar", "vector", "gpsimd", "tensor"):
            sems[e] = es.enter_context(nc.semaphore("c_" + e))
        pools = {e: [es.enter_context(nc.semaphore("d_%s%d" % (e, i))) for i in range(R)]
                 for e in ("sync", "gpsimd", "scalar")}
        CCR = 4
        ccsems = [es.enter_context(nc.semaphore("ccsem%d" % i)) for i in range(CCR)]
        cccount = 0
        for e in self.ENGS:
            cnt = 0
            dcnt = 0
            for op in self.ops[e]:
                if op.kind == "c":
                    if op.sig:
                        cnt += 1
                        op.sem = sems[e]
                        op.sigval = cnt
                elif op.kind == "d":
                    op.sem = pools[e][dcnt % R]
                    op.sigval = 16 * (dcnt // R + 1)
                    dcnt += 1
                else:
                    op.sem = ccsems[cccount % CCR]
                    op.sigval = cccount // CCR + 1
                    cccount += 1
        block = es.enter_context(nc.Block())
        prog = self

        def run(eng_name):
            def body(e):
                waited = {}
                for op in prog.ops[eng_name]:
                    need = {}
                    for d in op.deps:
                        key = id(d.sem)
                        if need.get(key, (None, 0))[1] < d.sigval:
                            need[key] = (d.sem, d.sigval)
                    if op.kind == "d" and op.sigval > 16:
                        key = id(op.sem)
                        if need.get(key, (None, 0))[1] < op.sigval - 16:
                            need[key] = (op.sem, op.sigval - 16)
                    if op.kind == "cc" and op.sigval > 1:
                        key = id(op.sem)
                        if need.get(key, (None, 0))[1] < op.sigval - 1:
                            need[key] = (op.sem, op.sigval - 1)
                    for key, (s, v) in need.items():
                        if waited.get(key, 0) >= v:
                            continue
                        e.wait_ge(s, v)
                        waited[key] = v
                    ins = op.fn(e)
                    if op.sig:
                        ins.then_inc(op.sem, 16 if op.kind == "d" else 1)
            return body

        block.sync(run("sync"))
        block.scalar(run("scalar"))
        block.vector(run("vector"))
        block.gpsimd(run("gpsimd"))
        block.tensor(run("tensor"))


def build_program(n_layers=L_ALL, stop=None):
    nc = bass.Bass("TRN2", target_bir_lowering=False)
    P = Prog()
    es = ExitStack()

    def din(name, shape, dt=F32):
        return nc.dram_tensor(name, list(shape), dt, kind="ExternalInput").ap()

    xT = din("xT", [D, NT])
    w_in = din("w_in", [L_ALL, D, NPROJ])
    w_out = din("w_out", [L_ALL, D, D])
    w_up = din("w_up", [L_ALL, D, 2 * DFF])
    w_down = din("w_down", [L_ALL, DFF, D])
    pp = din("pp", [128, NPP])
    ppl = din("ppl", [L_ALL, 128, 512])
    sgwT = din("sgwT", [L_ALL, 128, 512])
    rope = din("rope", [128, 2 * NT])
    rotm = din("rotm", [128, 128])
    yT = nc.dram_tensor("yT", [D, NT], F32, kind="ExternalOutput").ap()
    send1 = [nc.dram_tensor("send1_%d" % l, [128, NT], BF16).ap() for l in range(n_layers)]
    recv1 = [nc.dram_tensor("recv1_%d" % l, [4 * 128, NT], BF16).ap() for l in range(n_layers)]
    send1b = [nc.dram_tensor("send1b_%d" % l, [128, 2096], BF16).ap() for l in range(n_layers)]
    recv1b = [nc.dram_tensor("recv1b_%d" % l, [4 * 128, 2096], BF16).ap() for l in range(n_layers)]
    send2 = [nc.dram_tensor("send2_%d" % l, [128, 32], BF16).ap() for l in range(n_layers)]
    recv2 = [nc.dram_tensor("recv2_%d" % l, [4 * 128, 32], BF16).ap() for l in range(n_layers)]

    lscr = [nc.dram_tensor("lscr%d" % g, [1, 512], F32).ap() for g in range(2)]

    def sb(name, n, dt):
        return es.enter_context(nc.sbuf_tensor(name, [128, n], dt))

    X_t = sb("X", 8 * NT, F32)
    HB_t = sb("HB", 8192, BF16)
    AR_t = sb("AR", 22528, BF16)
    Q_t = sb("Q", 4 * NT, BF16)
    CB_t = sb("CB", 2 * NT, BF16)
    GO_t = sb("GO", 2 * NT, BF16)
    RC_t = sb("RC", 8192 + 32, BF16)
    PP_t = sb("PP", NPP, F32)
    PL_t = sb("PL", 512, F32)
    SGW_t = sb("SGW", 512, BF16)
    CST_t = sb("CST", 384, BF16)
    ONEF_t = sb("ONEF", 64, F32)
    BIAS_t = sb("BIAS", 4, F32)
    EPSB_t = sb("EPSB", 1, F32)
    TB_t = sb("TB", 4096, BF16)
    TF_t = sb("TF", 4608, F32)

    def V(t, off, *shape):
        n = int(np.prod(shape))
        a = t[:, off:off + n]
        if len(shape) == 1:
            return a
        if len(shape) == 2:
            return a.rearrange("p (a b) -> p a b", a=shape[0])
        if len(shape) == 3:
            return a.rearrange("p (a b c) -> p a b c", a=shape[0], b=shape[1])
        return a.rearrange("p (a b c d) -> p a b c d", a=shape[0], b=shape[1], c=shape[2])

    X = V(X_t, 0, 8, NT)
    Q = V(Q_t, 0, 4, NT)
    Qn = V(Q_t, 0, 16, 4, 128)
    CB = V(CB_t, 0, 2, NT)
    GO = V(GO_t, 0, 2, NT)
    ROPE = V(RC_t, 0, 2, NT)
    OT = V(RC_t, 0, 8, TC)
    CCX = V(RC_t, 4096, 2, NT + 2)
    EDG = V(RC_t, 8208, 4)
    EDG16 = V(RC_t, 8208, 16)
    ONES = CST_t[:, 0:128]
    BO = CST_t[:, 128:256]
    ROT = CST_t[:, 256:384]
    PPv = PP_t
    WI = V(AR_t, 0, 8, NPROJ)
    KO = V(AR_t, 16384, NT)
    VO = V(AR_t, 18432, 16, 2, 65)
    KA = V(AR_t, 0, 4, NT)
    VA = V(AR_t, 8192, 64, 2, 65)
    WOG = V(AR_t, 16512, 4, D)
    AT = V(AR_t, 0, NFF, 1024)
    WOA = V(HB_t, 0, 8, D)
    H = V(HB_t, 0, 2, 8, TC)
    H2 = V(HB_t, 0, 8, 1024)
    WU = V(Q_t, 0, 2, 2, 8, 256)
    WD = [V(CB_t, 0, NFF, 128), V(GO_t, 0, NFF, 128)]
    SU = V(TF_t, 3072, 2, 512)
    TSM = TF_t[:, 4104:4608]
    U = [V(TF_t, 0, 1026), V(TF_t, 1026, 1026)]
    CV = [V(TF_t, 2052, 1024), V(TF_t, 3076, 1024)]
    RCF = RC_t[:, 0:8208].bitcast(F32)
    U2 = [RCF[:, 0:1026], RCF[:, 1026:2052]]
    CV2 = [RCF[:, 2052:3076], RCF[:, 3076:4100]]
    RCK_F = [("U2", 0), ("U2", 1), ("CV2", 0), ("CV2", 1)]
    TFK = [("TF", i) for i in range(8)]
    UCK = [("U", 0), ("U", 1), ("CV", 0), ("CV", 1)]
    QCG = ["Q%d" % n for n in range(4)] + ["CBn%d" % n for n in range(4)] + ["GOn%d" % n for n in range(4)]
    WUD = [("WU", s_, g_) for s_ in range(2) for g_ in range(2)] + ["WD0", "WD1"]
    ARA = ["WI%d" % k for k in range(8)] + ["KO", "VO"]
    ARC = [("KA", r) for r in range(4)] + [("VA", r) for r in range(4)] + ["WOG"]

    def dma(eng, out, in_, r, w):
        return P.add(eng, lambda e: e.dma_start(out=out, in_=in_), r, w, kind="d")

    def mm(out, lhsT, rhs, start, stop, r, w):
        return P.add("tensor", lambda e: e.matmul(out, lhsT, rhs, start=start, stop=stop), r, w)

    def act(out, in_, func, r, w, bias=0.0, scale=1.0):
        return P.add("scalar", lambda e: e.activation(out=out, in_=in_, func=func, bias=bias, scale=scale), r, w)

    def tt(eng, out, a, b, op, r, w):
        return P.add(eng, lambda e: e.tensor_tensor(out=out, in0=a, in1=b, op=op), r, w)

    def ts(eng, out, a, s1, op0, r, w, s2=None, op1=None):
        if op1 is None:
            return P.add(eng, lambda e: e.tensor_scalar(out=out, in0=a, scalar1=s1, scalar2=None, op0=op0), r, w)
        return P.add(eng, lambda e: e.tensor_scalar(out=out, in0=a, scalar1=s1, scalar2=s2, op0=op0, op1=op1), r, w)

    def stt(out, a, s, b, op0, op1, r, w):
        return P.add("vector", lambda e: e.scalar_tensor_tensor(out=out, in0=a, scalar=s, in1=b, op0=op0, op1=op1), r, w)

    def recip(out, in_, r, w):
        return P.add("vector", lambda e: e.reciprocal(out=out, in_=in_), r, w)

    def cp(eng, out, in_, r, w):
        return P.add(eng, lambda e: e.tensor_copy(out=out, in_=in_), r, w)

    def memset(eng, ap, val, w):
        return P.add(eng, lambda e: e.memset(ap, val), (), w)

    def raw(eng, meth, r, w, kind="c", **kw):
        return P.add(eng, lambda e: getattr(e, meth)(**kw), r, w, kind=kind)

    PSALL = es.enter_context(nc.psum_tensor("psall", [128, 4096], F32))
    PS = [PSALL[:, i * 512:(i + 1) * 512] for i in range(8)]

    def pk(i):
        return ("ps", i)

    import os as _os
    for _ in range(int(_os.environ.get("KDUMMY", "0"))):
        P.add("sync", lambda e: e.nop(), (), ())
    dma("sync", PP_t[:, :], pp, (), ["PP"])
    dma("gpsimd", ROT, rotm, (), ["CST"])
    memset("vector", CST_t[:, 0:128], 1.0, ["CST1"])
    memset("vector", CST_t[:, 128:256], 0.0, ["CST2"])
    memset("vector", CST_t[0:64, 128:192], 1.0, ["CST2"])
    memset("vector", CST_t[64:128, 192:256], 1.0, ["CST2"])
    memset("vector", ONEF_t[:, :], 1.0, ["ONEF"])
    memset("vector", EPSB_t[:, :], EPS, ["EPSB"])
    memset("vector", TB_t[:, 2048:3072], 0.0, [("TB", 4), ("TB", 5)])
    memset("vector", TB_t[:, 3072:3584], 0.0, ["H2E"])
    memset("vector", EDG16, 0.0, ["EDG"])
    for c in range(8):
        dma("sync", X[:, c, :], xT[c * 128:(c + 1) * 128, :], (), [("X", c, n) for n in range(4)])
    P.consts.update(["PP", "CST", "CST1", "CST2", "ONEF", "EPSB"])

    tbn = [0]

    def tb():
        i = tbn[0] % 4
        tbn[0] += 1
        return TB_t[:, i * 512:(i + 1) * 512], ("TB", i)

    tfn = [0]

    def tf():
        i = tfn[0] % 6
        tfn[0] += 1
        return TF_t[:, i * 512:(i + 1) * 512], ("TF", i)

    def rms_sumsq(psb, src_fn, nchunk, lhsT, rkeys):
        for c in range(nchunk):
            sq, sqk = tb()
            src, sk = src_fn(c)
            act(sq, src, AF.Square, sk, [sqk])
            mm(PS[psb][:, :], lhsT, sq, c == 0, c == nchunk - 1, [sqk] + rkeys, [pk(psb)])

    def rstd_from(psb, inv_n, buf=None):
        rs, rk = buf if buf is not None else tf()
        act(rs, PS[psb][:, :], AF.Ln, [pk(psb), "EPSB"], [rk], bias=EPSB_t[:, 0:1], scale=inv_n)
        act(rs, rs, AF.Exp, [rk], [rk], scale=-0.5)
        return rs, rk

    def norm_chunk(l, goff, tok0, dst_fn, dstkeys, buf=None):
        n = tok0 // TC
        rms_sumsq(0, lambda c: (X[:, c, tok0:tok0 + TC], [("X", c, n)]), 8, ONES, ["CST1"])
        rs, rk = rstd_from(0, 1.0 / D, buf)
        for c in range(8):
            stt(dst_fn(c), X[:, c, tok0:tok0 + TC], PPv[:, goff + l * 8 + c:goff + l * 8 + c + 1], rs,
                ALU.mult, ALU.mult, [("X", c, n), rk, "PP"], dstkeys(c))

    for l in range(n_layers if stop != 'setup' else 0):
        P.alias(ARA, ["AT"] + ARC)
        P.alias([("H", 0), ("H", 1)], ["H2", "WOA"])
        P.alias(QCG, WUD)
        P.alias(TFK, UCK)
        P.alias([("ROPE", 0), ("ROPE", 1), "CCX"], ["OT", "CCX"] + RCK_F)
        for k in range(8):
            dma("gpsimd", WI[:, k, :], w_in[l, k * 128:(k + 1) * 128, :], (), ["WI%d" % k])
        dma("sync", PL_t[:, :], ppl[l], (), ["PL"])
        dma("gpsimd", SGW_t[:, :], sgwT[l], (), ["SGW"])
        dma("gpsimd", RC_t[:, 0:NT], rope[:, 0:NT], (), [("ROPE", 0)])
        dma("gpsimd", RC_t[:, NT:2 * NT], rope[:, NT:2 * NT], (), [("ROPE", 1)])
        RK = [("ROPE", 0), ("ROPE", 1)]
        VOf = AR_t[:, 18432:18432 + 16 * 130].rearrange("p (a c) -> p a c", c=65)
        memset("vector", VOf[:, :, 64:65], 1.0, ["VO"])
        qkg = PPv[:, O_QKG + l * 128:O_QKG + (l + 1) * 128].rearrange("p (a b) -> p a b", a=2)
        raw("vector", "tensor_reduce", ["PP"], ["TSM"], out=TSM[:, 0:2], in_=qkg, axis=AX.X, op=ALU.max,
            apply_absolute_value=True)
        ts("vector", BIAS_t[:, l:l + 1], TSM[:, 0:1], TSM[:, 1:2], ALU.mult, ["TSM"], [("BIAS", l)], s2=-8.0,
           op1=ALU.mult)

        O_GV = NPP - 8

        def send_block():
            for i in range(2):
                cp("vector", EDG[:, 2 * i:2 * i + 1], CCX[:, i, 1:2], ["CCX"], ["EDG"])
                cp("vector", EDG[:, 2 * i + 1:2 * i + 2], CCX[:, i, NT:NT + 1], ["CCX"], ["EDG"])
            dma("sync", send1[l], KO, ["KO"], [("S1", l, 0)])
            dma("sync", send1b[l][:, 0:2080], AR_t[:, 18432:18432 + 2080], ["VO"], [("S1", l, 1)])
            dma("sync", send1b[l][:, 2080:2096], EDG16, ["EDG"], [("S1", l, 2)])
            P.add("gpsimd", lambda e, l=l: e.collective_compute("AllGather", ALU.bypass, replica_groups=GROUPS,
                                                                ins=[send1[l]], outs=[recv1[l]]),
                  [("S1", l, 0)], [("R1", l)], kind="cc")
            P.add("gpsimd", lambda e, l=l: e.collective_compute("AllGather", ALU.bypass, replica_groups=GROUPS,
                                                                ins=[send1b[l]], outs=[recv1b[l]]),
                  [("S1", l, 1), ("S1", l, 2)], [("R1b", l)], kind="cc")

        for n in range(4):
            tok0 = n * TC
            hb = n % 2
            hk = ("H", hb)
            if n == 0:
                norm_chunk(l, O_G1, tok0, lambda c: H[:, hb, c, :], lambda c: [hk])
            cct = [None, None]
            st = {}
            pbc = [0]

            def stage0(j):
                pbc[0] += 1
                pb = 1 + (pbc[0] % 2)
                for k in range(8):
                    mm(PS[pb][:, :], WI[:, k, j * 128:(j + 1) * 128], H[:, hb, k, :], k == 0, k == 7,
                       [hk, "WI%d" % k], [pk(pb)])
                if j <= 4:
                    qf, qfk = tf()
                    act(qf, PS[pb][:, :], AF.Copy, [pk(pb)], [qfk])
                    sq, sqk = tb()
                    act(sq, qf, AF.Square, [qfk], [sqk])
                    st[j] = dict(qf=qf, qfk=qfk, sq=sq, sqk=sqk)
                elif j <= 6:
                    act(SU[:, j - 5, :], PS[pb][:, :], AF.Copy, [pk(pb)], [("TF", 6 + j - 5)])
                elif j <= 8:
                    act(CB[:, j - 7, tok0:tok0 + TC], PS[pb][:, :], AF.Copy, [pk(pb)], ["CBn%d" % n])
                elif j <= 10:
                    c_, ck_ = tf()
                    act(c_, PS[pb][:, :], AF.Copy, [pk(pb)], [ck_])
                    cct[j - 9] = (c_, ck_)
                else:
                    i = j - 11
                    c_, ck_ = cct[i]
                    tt("vector", CCX[:, i, 1 + tok0:1 + tok0 + TC], PS[pb][:, :], c_, ALU.mult,
                       [pk(pb), ck_], ["CCX"])

            def stage1(j):
                d_ = st[j]
                gcol = (O_GQ if j < 4 else O_GK) + l
                mm(PS[3][:, :], BO, d_["sq"], True, True, [d_["sqk"], "CST2"], [pk(3)])
                rs, rk = rstd_from(3, 1.0 / 64)
                qn, qnk = tb()
                stt(qn, d_["qf"], PPv[:, gcol:gcol + 1], rs, ALU.mult, ALU.mult, [d_["qfk"], rk, "PP"], [qnk])
                d_["qn"], d_["qnk"] = qn, qnk

            def stage2(j):
                d_ = st[j]
                qn, qnk = d_["qn"], d_["qnk"]
                mm(PS[4][:, :], ROT, qn, True, True, [qnk, "CST"], [pk(4)])
                t1, t1k = tf()
                tt("gpsimd", t1, qn, ROPE[:, 0, tok0:tok0 + TC], ALU.mult, [qnk] + RK, [t1k])
                t2, t2k = tf()
                tt("vector", t2, PS[4][:, :], ROPE[:, 1, tok0:tok0 + TC], ALU.mult, [pk(4)] + RK, [t2k])
                if j < 4:
                    tt("vector", Qn[:, n * 4:(n + 1) * 4, j, :], t1.rearrange("p (b q) -> p b q", b=4),
                       t2.rearrange("p (b q) -> p b q", b=4), ALU.add, [t1k, t2k], ["Q%d" % n])
                else:
                    tt("vector", KO[:, tok0:tok0 + TC], t1, t2, ALU.add, [t1k, t2k], ["KO"])

            def run_steps(order):
                L_ = len(order)
                for s_ in range(L_ + 2):
                    if s_ < L_:
                        stage0(order[s_])
                    if 0 <= s_ - 1 < L_ and order[s_ - 1] <= 4:
                        stage1(order[s_ - 1])
                    if 0 <= s_ - 2 < L_ and order[s_ - 2] <= 4:
                        stage2(order[s_ - 2])

            run_steps([4, 9, 10, 11, 12])
            def tm_a(t4):
                tile = n * 4 + t4
                tmb = 5 if tile % 2 == 0 else 0
                for k in range(8):
                    mm(PS[tmb][:, 0:384], H[:, hb, k, t4 * 128:(t4 + 1) * 128], WI[:, k, 1664:2048], k == 0, k == 7,
                       [hk, "WI%d" % k], [pk(tmb)])
                sq, sqk = tf()
                act(sq[:, 0:256], PS[tmb][:, 128:384], AF.Square, [pk(tmb)], [sqk])
                act(VO[:, tile, :, 0:64], PS[tmb][:, 0:128].rearrange("p (g c) -> p g c", g=2), AF.Copy,
                    [pk(tmb)], ["VO"])
                ssk = ("SS", t4 % 2)
                ss = TSM[:, 8 + (t4 % 2) * 4: 9 + (t4 % 2) * 4]
                raw("vector", "tensor_reduce", [sqk], [ssk], out=ss, in_=sq[:, 0:256], axis=AX.X, op=ALU.add)
                act(ss, ss, AF.Ln, [ssk, "EPSB"], [ssk], bias=EPSB_t[:, 0:1], scale=1.0 / 256)
                act(ss, ss, AF.Exp, [ssk], [ssk], scale=-0.5)
                vb = tile % 2
                vnp = V(TB_t, 2048 + vb * 512, 2, 2, 128)
                vnk = ("TB", 4 + vb)
                for a in range(2):
                    ts("vector", vnp[:, :, a, a * 64:(a + 1) * 64],
                       PS[tmb][:, 128:384].rearrange("p (i a c) -> p i a c", i=2, a=2)[:, :, a, :], ss,
                       ALU.mult, [pk(tmb), ssk], [vnk])

            def tm_b(t4):
                tile = n * 4 + t4
                vb = tile % 2
                vnp = V(TB_t, 2048 + vb * 512, 2, 2, 128)
                vnk = ("TB", 4 + vb)
                for i in range(2):
                    for a in range(2):
                        mm(PS[6 + i][:, t4 * 128:(t4 + 1) * 128], vnp[:, i, a, :],
                           SGW_t[:, (2 * i + a) * 128:(2 * i + a + 1) * 128], a == 0, a == 1,
                           [vnk, "SGW"], [pk(6 + i)])

            for t4 in range(5):
                if t4 < 4:
                    tm_a(t4)
                if t4 >= 1:
                    tm_b(t4 - 1)
            if n == 3:
                send_block()
            else:
                nb = (n + 1) % 2
                norm_chunk(l, O_G1, (n + 1) * TC, lambda c, nb=nb: H[:, nb, c, :], lambda c, nb=nb: [("H", nb)])
            run_steps([0, 1, 2, 3, 5, 6, 7, 8])
            for i in range(2):
                t_, tk_ = tf()
                b2 = PL_t[:, i * 128:(i + 1) * 128]
                gvs = PPv[:, O_GV + l * 2 + i:O_GV + l * 2 + i + 1]
                for t4 in range(4):
                    stt(t_[:, t4 * 128:(t4 + 1) * 128], PS[6 + i][:, t4 * 128:(t4 + 1) * 128], gvs, b2,
                        ALU.mult, ALU.add, [pk(6 + i), "PL", "PP"], [tk_])
                tt("vector", GO[:, i, tok0:tok0 + TC], t_, SU[:, i, :], ALU.mult, [tk_, ("TF", 6 + i)],
                   ["GOn%d" % n])

        if stop == 'A':
            break
        P.alias(ARC, ARA)
        r1 = recv1[l].rearrange("(r p) c -> p r c", p=128)
        r1b = recv1b[l].rearrange("(r p) c -> p r c", p=128)
        for r in range(4):
            dma("sync", KA[:, r, :], r1[:, r, :], [("R1", l)], [("KA", r)])
            dma("scalar", AR_t[:, 8192 + r * 2080: 8192 + (r + 1) * 2080], r1b[:, r, 0:2080],
                [("R1b", l)], [("VA", r)])
        E4 = V(TB_t, 3072, 4, 4)
        dma("sync", E4, r1b[:, :, 2080:2084], [("R1b", l)], ["E4"])
        if stop == 'B2':
            break
        P.alias(["WOA"], [("H", 0), ("H", 1)])
        wo = w_out[l]
        dma("gpsimd", WOA[0:64, :, :], wo[0:512, :].rearrange("(h p) d -> p h d", p=64), (), ["WOA"])
        dma("gpsimd", WOG, wo[512:1024, :].rearrange("(c p) d -> p c d", p=128), (), ["WOG"])
        if stop == 'B3':
            break
        hal = TSM[:, 16:20]
        for side in range(2):
            e_idx = 1 if side == 0 else 0

            def src(r, e_idx=e_idx):
                return E4[:, r, :].rearrange("p (i e) -> p i e", i=2)[:, :, e_idx]
            dst = hal[:, side * 2:side * 2 + 2]
            mcol = O_MASK + side * 4
            ts("vector", dst, src(0), PPv[:, mcol:mcol + 1], ALU.mult, ["E4", "PP"], ["HAL"])
            for r in range(1, 4):
                stt(dst, src(r), PPv[:, mcol + r:mcol + r + 1], dst, ALU.mult, ALU.add, ["E4", "PP", "HAL"], ["HAL"])
        for i in range(2):
            cp("vector", CCX[:, i, 0:1], hal[:, i:i + 1], ["HAL"], ["CCX"])
            cp("vector", CCX[:, i, NT + 1:NT + 2], hal[:, 2 + i:3 + i], ["HAL"], ["CCX"])
        for n in range(4):
            tok0 = n * TC
            for i in range(2):
                cw = O_CW + l * 6 + i * 3
                t_, tk_ = tf()
                ts("vector", t_, CCX[:, i, tok0:tok0 + TC], PPv[:, cw:cw + 1], ALU.mult, ["CCX", "PP"], [tk_])
                stt(t_, CCX[:, i, tok0 + 1:tok0 + 1 + TC], PPv[:, cw + 1:cw + 2], t_, ALU.mult, ALU.add,
                    ["CCX", "PP", tk_], [tk_])
                stt(t_, CCX[:, i, tok0 + 2:tok0 + 2 + TC], PPv[:, cw + 2:cw + 3], t_, ALU.mult, ALU.add,
                    ["CCX", "PP", tk_], [tk_])
                tt("vector", CB[:, i, tok0:tok0 + TC], CB[:, i, tok0:tok0 + TC], t_, ALU.mult,
                   ["CBn%d" % n, tk_], ["CBn%d" % n])

        if stop == 'B':
            break
        P.alias(["OT"], RK)

        ecols = [0, 1023, 1024, 2047]

        def edge_norm(es, slot, psb):
            o = 128 + slot * 96
            XE = TSM[:, o:o + 32].rearrange("p (c e) -> p c e", c=8)
            xg = TSM[:, o + 32:o + 64].rearrange("p (c e) -> p c e", c=8)
            rse = TSM[:, o + 64:o + 68]
            sqe = TB_t[:, 3312 + slot * 32:3344 + slot * 32]
            H2E_ = TB_t[:, 3120:3152].rearrange("p (c e) -> p c e", c=8)
            kx, ks, kr, kg = ("XE", slot), ("SQE", slot), ("RSE", slot), ("XG", slot)
            memset("vector", TSM[:, o:o + 32], 1.0, [kx])
            for e_i in es:
                ec = ecols[e_i]
                cp("vector", XE[:, :, e_i], X[:, :, ec], [("X", c, ec // TC) for c in range(8)], [kx])
            act(sqe, TSM[:, o:o + 32], AF.Square, [kx], [ks])
            sqe3 = sqe.rearrange("p (c e) -> p c e", c=8)
            for c in range(8):
                mm(PS[psb][:, 0:4], ONES, sqe3[:, c, :], c == 0, c == 7, [ks, "CST1"], [pk(psb)])
            act(rse, PS[psb][:, 0:4], AF.Ln, [pk(psb), "EPSB"], [kr], bias=EPSB_t[:, 0:1], scale=1.0 / D)
            act(rse, rse, AF.Exp, [kr], [kr], scale=-0.5)
            g2 = PPv[:, O_G2 + l * 8:O_G2 + l * 8 + 8]
            for e_i in es:
                tt("vector", xg[:, :, e_i], XE[:, :, e_i], g2, ALU.mult, [kx, "PP"], [kg])
                ts("vector", H2E_[:, :, e_i], xg[:, :, e_i], rse[:, e_i:e_i + 1], ALU.mult, [kg, kr], ["H2E"])

        def edge_exchange():
            edge_norm([0, 3], 0, 7)
            dma("sync", send2[l], TB_t[:, 3120:3152], ["H2E"], [("S2", l)])
            P.add("gpsimd", lambda e, l=l: e.collective_compute("AllGather", ALU.bypass, replica_groups=GROUPS,
                                                                ins=[send2[l]], outs=[recv2[l]]),
                  [("S2", l)], [("R2", l)], kind="cc")
            dma("sync", TB_t[:, 3152:3280].rearrange("p (r c) -> p r c", r=4),
                recv2[l].rearrange("(r p) c -> p r c", p=128), [("R2", l)], ["E2"])

        pend = []

        def wout_closures(n, d):
            tok0 = n * TC
            pb = 7
            cl = []
            for h in range(8):
                cl.append(lambda h=h: mm(PS[pb][:, :], WOA[0:64, h, d * 128:(d + 1) * 128], OT[0:64, h, :],
                                         h == 0, False, ["WOA", "OT"], [pk(pb)]))
            for i in range(2):
                cl.append(lambda i=i: mm(PS[pb][:, :], WOG[:, i, d * 128:(d + 1) * 128], GO[:, i, tok0:tok0 + TC],
                                         False, False, ["WOG", "GOn%d" % n], [pk(pb)]))

            def last(i):
                mm(PS[pb][:, :], WOG[:, 2 + i, d * 128:(d + 1) * 128], CB[:, i, tok0:tok0 + TC], False, i == 1,
                   ["WOG", "CBn%d" % n], [pk(pb)])
                if i == 1:
                    tt("vector", X[:, d, tok0:tok0 + TC], X[:, d, tok0:tok0 + TC], PS[pb][:, :], ALU.add,
                       [("X", d, n), pk(pb)], [("X", d, n)])
            for i in range(2):
                cl.append(lambda i=i: last(i))
            return cl

        LB = V(TF_t, 1024, 2, 512)

        def obank(g, grp):
            return 5 if g == 1 else (4 if grp % 2 == 0 else 6)

        def evac(n, qb, grp=0):
            for g in (1, 0):
                os_ = TF_t[:, g * 512:(g + 1) * 512]
                ob = obank(g, grp)
                cp("vector", os_[0:65, :], PS[ob][0:65, :], [pk(ob)], [("TF", g)])
            for g in range(2):
                os_ = TF_t[:, g * 512:(g + 1) * 512]
                osk = ("TF", g)
                lbk = ("TF", 2 + g)
                dma("sync", lscr[g], os_[64:65, :], [osk], [("LD", g)])
                dma("sync", LB[0:64, g, :], lscr[g].partition_broadcast(64), [("LD", g)], [lbk])
                recip(LB[0:64, g, :], LB[0:64, g, :], [lbk], [lbk])
                tt("vector", OT[0:64, 4 * g:4 * g + 4, qb * 128:(qb + 1) * 128],
                   os_[0:64, :].rearrange("p (h q) -> p h q", h=4),
                   LB[0:64, g, :].rearrange("p (h q) -> p h q", h=4), ALU.mult, [osk, lbk], ["OT"])

        items = [(n, qb, kt) for n in (0, 3, 1, 2) for qb in range(4) for kt in range(64)]

        def emit_qk(it):
            n, qb, kt = items[it]
            q0 = n * TC + qb * 128
            sp = it % 2
            for g in range(2):
                mm(PS[sp * 2 + g][:, :].rearrange("p (h q) -> p h q", h=4),
                   KA[64 * g:64 * g + 64, kt // 16, (kt % 16) * 128:(kt % 16 + 1) * 128],
                   Qn[64 * g:64 * g + 64, n * 4 + qb, :, :], True, True, [("KA", kt // 16), "Q%d" % n],
                   [("sp", sp)])

        PTB = [TB_t[:, 0:1024], TB_t[:, 1024:2048], TF_t[:, 2048:2560].bitcast(BF16)]
        PTK = [[("TB", 0), ("TB", 1)], [("TB", 2), ("TB", 3)], [("TF", 4)]]

        def emit_exp(it):
            sp = it % 2
            pb_ = it % 3
            act(PTB[pb_], PSALL[:, sp * 1024:(sp + 1) * 1024], AF.Exp, [("sp", sp)], PTK[pb_], scale=0.125)

        def emit_pv(it):
            n, qb, kt = items[it]
            pb_ = it % 3
            grp_ = it // 64
            for g in range(2):
                ob = obank(g, grp_)
                mm(PS[ob][0:65, :], VA[:, kt, g, :], PTB[pb_][:, g * 512:(g + 1) * 512],
                   kt == 0, kt == 63, [("VA", kt // 16)] + PTK[pb_], [pk(ob)])
            if kt == 63:
                while pend:
                    pend.pop(0)()
                evac(n, qb, it // 64)
                if qb == 3:
                    for d in range(8):
                        pend.extend(wout_closures(n, d))
                    if n == 3:
                        pend.append(edge_exchange)

        P.alias([("sp", 0), ("sp", 1)], [pk(0), pk(1), pk(2), pk(3)])
        nit = len(items)
        emit_qk(0)
        emit_qk(1)
        emit_exp(0)
        emit_exp(1)
        for it in range(nit):
            if it + 2 < nit:
                emit_qk(it + 2)
            emit_pv(it)
            if it + 2 < nit:
                emit_exp(it + 2)
            kt = items[it][2]
            if 14 <= kt < 63:
                for _ in range(2):
                    if pend:
                        pend.pop(0)()
        while pend:
            pend.pop(0)()
        P.alias([pk(0), pk(1), pk(2), pk(3)], [("sp", 0), ("sp", 1)])

        if stop == 'C':
            break
        P.alias(["H2"], ["WOA"])
        P.alias(["AT"], ARC)
        P.alias(WUD, QCG)
        P.alias(UCK, TFK)
        P.alias(RCK_F, ["OT", "CCX"])
        wu_list = [(hf, b) for hf in range(2) for b in range(11)]
        wd_list = [(hf, d) for hf in range(2) for d in range(8)]
        wu_next = [0]
        wd_next = [0]

        def issue_wu(upto):
            while wu_next[0] <= upto and wu_next[0] < len(wu_list):
                q_ = wu_next[0]
                hf_, b_ = wu_list[q_]
                st_ = q_ % 2
                for gv in range(2):
                    col0 = gv * DFF + b_ * 256
                    dma("gpsimd", WU[:, st_, gv, :, :],
                        w_up[l, :, col0:col0 + 256].rearrange("(k p) n -> p k n", p=128), (), [("WU", st_, gv)])
                wu_next[0] += 1

        def issue_wd(upto):
            while wd_next[0] <= upto and wd_next[0] < len(wd_list):
                q_ = wd_next[0]
                hf_, d_ = wd_list[q_]
                st_ = q_ % 2
                dma("gpsimd", WD[st_], w_down[l, :, d_ * 128:(d_ + 1) * 128].rearrange("(j p) n -> p j n", p=128),
                    (), ["WD%d" % st_])
                wd_next[0] += 1

        issue_wu(1)
        issue_wd(1)
        edge_norm([1, 2], 1, 0)
        E2 = TB_t[:, 3152:3280].rearrange("p (r c e) -> p r c e", r=4, c=8)
        H2E = TB_t[:, 3120:3152].rearrange("p (c e) -> p c e", c=8)
        halh = TSM[:, 96:112].rearrange("p (s c) -> p s c", s=2)
        for side in range(2):
            e_idx = 3 if side == 0 else 0
            mcol = O_MASK + side * 4
            ts("vector", halh[:, side, :], E2[:, 0, :, e_idx], PPv[:, mcol:mcol + 1], ALU.mult, ["E2", "PP"], ["HALH"])
            for r in range(1, 4):
                stt(halh[:, side, :], E2[:, r, :, e_idx], PPv[:, mcol + r:mcol + r + 1], halh[:, side, :],
                    ALU.mult, ALU.add, ["E2", "PP", "HALH"], ["HALH"])
        HH = TB_t[:, 3280:3312].rearrange("p (f k e) -> p f k e", f=2, k=8)
        cp("vector", HH[:, 0, :, 0], halh[:, 0, :], ["HALH"], ["HH"])
        cp("vector", HH[:, 0, :, 1], H2E[:, :, 2], ["H2E"], ["HH"])
        cp("vector", HH[:, 1, :, 0], H2E[:, :, 1], ["H2E"], ["HH"])
        cp("vector", HH[:, 1, :, 1], halh[:, 1, :], ["HALH"], ["HH"])

        epi_pend = []
        def ffn_norm(hf_):
            for t in range(2):
                norm_chunk(l, O_G2, hf_ * 1024 + t * TC, lambda c, t=t: H2[:, c, t * TC:(t + 1) * TC],
                           lambda c: ["H2"], buf=(CV[1][:, 0:512], ("CV", 1)))

        ffn_norm(0)
        for hf in range(2):
            h0 = hf * 1024
            for b in range(11):
                q_ = hf * 11 + b
                st = q_ % 2
                issue_wu(q_ + 1)
                for jj in range(2):
                    j = 2 * b + jj
                    for gv in range(2):
                        wuk = ("WU", st, gv)
                        for t in range(2):
                            pb = 1 + gv * 2 + t
                            for k in range(8):
                                mm(PS[pb][:, :], WU[:, st, gv, k, jj * 128:(jj + 1) * 128],
                                   H2[:, k, t * TC:(t + 1) * TC], k == 0, k == 7, [wuk, "H2"], [pk(pb)])
                        hbk = 5 if j % 2 == 0 else 0
                        for k in range(8):
                            mm(PS[hbk][:, gv * 2:gv * 2 + 2], WU[:, st, gv, k, jj * 128:(jj + 1) * 128],
                               HH[:, hf, k, :], k == 0, k == 7, [wuk, "HH"], [pk(hbk)])
                    bs = j % 2
                    Ub, CVb = (U, CV) if bs == 0 else (U2, CV2)
                    for gv in range(2):
                        uk = ("U", gv) if bs == 0 else ("U2", gv)
                        act(Ub[gv][:, 1:513], PS[1 + gv * 2][:, :], AF.Copy, [pk(1 + gv * 2)], [uk])
                        act(Ub[gv][:, 513:1025], PS[2 + gv * 2][:, :], AF.Copy, [pk(2 + gv * 2)], [uk])
                        hbk = 5 if j % 2 == 0 else 0
                        cp("vector", Ub[gv][:, 0:1], PS[hbk][:, gv * 2:gv * 2 + 1], [pk(hbk)], [uk])
                        cp("vector", Ub[gv][:, 1025:1026], PS[hbk][:, gv * 2 + 1:gv * 2 + 2], [pk(hbk)], [uk])
                        fw = O_FCW + (l * 44 + gv * 22 + j) * 3
                        ck = ("CV", gv) if bs == 0 else ("CV2", gv)
                        act(CVb[gv], Ub[gv][:, 0:1024], AF.Copy, [uk, "PP"], [ck], scale=PPv[:, fw:fw + 1])
                        stt(CVb[gv], Ub[gv][:, 1:1025], PPv[:, fw + 1:fw + 2], CVb[gv], ALU.mult, ALU.add,
                            [uk, "PP", ck], [ck])
                        stt(CVb[gv], Ub[gv][:, 2:1026], PPv[:, fw + 2:fw + 3], CVb[gv], ALU.mult, ALU.add,
                            [uk, "PP", ck], [ck])
                    ck0 = ("CV", 0) if bs == 0 else ("CV2", 0)
                    ck1 = ("CV", 1) if bs == 0 else ("CV2", 1)

                    def epi(CVb=CVb, ck0=ck0, ck1=ck1, j=j):
                        act(CVb[0], CVb[0], AF.Silu, [ck0], [ck0])
                        tt("vector", AT[:, j, :], CVb[0], CVb[1], ALU.mult, [ck0, ck1], ["AT"])
                    if epi_pend:
                        epi_pend.pop(0)()
                    epi_pend.append(epi)
            while epi_pend:
                epi_pend.pop(0)()
            for d in range(8):
                if hf == 0 and d == 1:
                    ffn_norm(1)
                q_ = hf * 8 + d
                st = q_ % 2
                wdk = "WD%d" % st
                issue_wd(q_ + 1)
                for t in range(2):
                    pb = 6 + t
                    for j in range(NFF):
                        mm(PS[pb][:, :], WD[st][:, j, :], AT[:, j, t * TC:(t + 1) * TC], j == 0, j == NFF - 1,
                           [wdk, "AT"], [pk(pb)])
                    n = (h0 + t * TC) // TC
                    tt("vector", X[:, d, h0 + t * TC:h0 + (t + 1) * TC], X[:, d, h0 + t * TC:h0 + (t + 1) * TC],
                       PS[pb][:, :], ALU.add, [("X", d, n), pk(pb)], [("X", d, n)])

    for c in range(8):
        dma("sync", yT[c * 128:(c + 1) * 128, :], X[:, c, :], [("X", c, n) for n in range(4)], [("Y", c)])
    P.add("sync", lambda e: e.nop(), [("Y", c) for c in range(8)], ["FIN"])

    P.emit(nc, es)
    es.close()
    return nc


_CACHE = {}


def _rope_tables(core):
    pos0 = (core % 4) * NT
    t = np.arange(NT) + pos0
    row = (t // 64).astype(np.float32)
    col = (t % 64).astype(np.float32)
    inv = (1.0 / (10000.0 ** (np.arange(16, dtype=np.float32) * 2.0 / 32.0))).astype(np.float32)
    ang = np.zeros((128, NT), np.float32)
    for p in range(128):
        d = p % 64
        if d < 32:
            ang[p] = row * inv[d % 16]
        else:
            ang[p] = col * inv[(d - 32) % 16]
    return np.concatenate([np.cos(ang), np.sin(ang)], axis=1).astype(np.float32)


def _rotm():
    m = np.zeros((128, 128), np.float32)
    for hb in (0, 64):
        for base in (0, 32):
            for i in range(16):
                m[hb + base + 16 + i, hb + base + i] = -1.0
                m[hb + base + i, hb + base + 16 + i] = 1.0
    return m


def kernel(x, norm1_g, w_in, q_norm_g, k_norm_g, sg_norm_g, sg_w, sg_b, conv_w, w_out, norm2_g,
           ffn_w_up, ffn_conv_w, ffn_w_down, _n_layers=L_ALL, _stop=None):
    f = lambda a: np.ascontiguousarray(np.asarray(a, dtype=np.float32))
    x = f(x)
    L = L_ALL
    qcols = []
    for c in range(4):
        qcols += list(range(c * 64, c * 64 + 64)) + list(range((4 + c) * 64, (4 + c) * 64 + 64))
    perm = (qcols + list(range(512, 640)) + list(range(768, 1024)) + list(range(1280, 1536))
            + list(range(1536, 1792)) + list(range(1792, 2048)) + list(range(640, 768)) + list(range(1024, 1280)))
    w_in_p = f(np.asarray(w_in)[:, :, perm])
    w_out_f, w_up_f, w_down_f = f(w_out), f(ffn_w_up), f(ffn_w_down)
    n1, n2 = f(norm1_g), f(norm2_g)
    qg, kg, sgg, sgb, cw, fcw = f(q_norm_g), f(k_norm_g), f(sg_norm_g), f(sg_b), f(conv_w), f(ffn_conv_w)
    pp0 = np.zeros((128, NPP), np.float32)
    pp0[:, O_G1:O_G1 + 32] = n1.reshape(L, 8, 128).transpose(2, 0, 1).reshape(128, 32)
    pp0[:, O_G2:O_G2 + 32] = n2.reshape(L, 8, 128).transpose(2, 0, 1).reshape(128, 32)
    pidx = np.arange(128) % 64
    pp0[:, O_GQ:O_GQ + L] = qg[:, pidx].T
    pp0[:, O_GK:O_GK + L] = kg[:, pidx].T
    pp0[:, O_CW:O_CW + 24] = cw.reshape(L, 3, 2, 128).transpose(3, 0, 2, 1).reshape(128, 24)
    pp0[:, O_FCW:O_FCW + 528] = fcw.reshape(L, 3, 44, 128).transpose(3, 0, 2, 1).reshape(128, 528)
    qk = np.stack([qg, kg], axis=1).reshape(1, L * 128)
    pp0[:, O_QKG:O_QKG + L * 128] = np.broadcast_to(qk, (128, L * 128))
    pp0[:, NPP - 8:NPP] = sgg.reshape(L, 2, 128).transpose(2, 0, 1).reshape(128, 8)
    ppl = np.zeros((L, 128, 512), np.float32)
    for l in range(L):
        for i in range(2):
            for a in range(2):
                ppl[l, a * 64:(a + 1) * 64, i * 128:(i + 1) * 128] = sgb[l, 2 * i + a][None, :]
        ppl[l, :, 256:512] = sgg[l][None, :]
    sgwT = f(np.asarray(sg_w).transpose(0, 3, 1, 2).reshape(L, 128, 512))
    rot = _rotm()

    key = (_n_layers, _stop)
    if key not in _CACHE:
        _CACHE[key] = build_program(_n_layers, _stop)
    nc = _CACHE[key]

    in_maps = []
    for r in range(NCORE):
        b, s0 = r // 4, (r % 4) * NT
        pp_r = pp0.copy()
        if r % 4 > 0:
            pp_r[:, O_MASK + (r % 4) - 1] = 1.0
        if r % 4 < 3:
            pp_r[:, O_MASK + 4 + (r % 4) + 1] = 1.0
        in_maps.append({
            "xT": np.ascontiguousarray(x[b, s0:s0 + NT, :].T),
            "w_in": w_in_p, "w_out": w_out_f, "w_up": w_up_f, "w_down": w_down_f,
            "pp": pp_r, "ppl": ppl, "sgwT": sgwT, "rope": _rope_tables(r), "rotm": rot,
        })
    res = run_bass_kernel_spmd(nc, in_maps, core_ids=list(range(NCORE)))
    out = np.empty((2, 8192, D), np.float32)
    for r in range(NCORE):
        b, s0 = r // 4, (r % 4) * NT
        out[b, s0:s0 + NT, :] = np.asarray(res.results[r]["yT"]).T
    return out
```
